# Optimizing a Trainium2 kernel written in Bass

```python
import jax
import jax.numpy as jnp
from jax import lax
import numpy as np

D_MODEL = 1024
BATCH = 4
SEQ = 4096
DEPTH = 4

SSM_WIDTH = D_MODEL // 2
SSM_GROUP = 16
SSM_GROUPS = SSM_WIDTH // SSM_GROUP
SSM_STATE = 64
DT_MIN = 1e-3
DT_MAX = 1e-1
CONV_CH = D_MODEL // 2
CONV_K = 3
HEAD_DIM = 64
N_HEADS = (D_MODEL // 2) // HEAD_DIM
N_KV_HEADS = 2
GQA = N_HEADS // N_KV_HEADS
ATTN_WIDTH = N_HEADS * HEAD_DIM
KV_WIDTH = N_KV_HEADS * HEAD_DIM
CMP_BLOCK = 32
SEL_BLOCK = 64
N_SELECT = 16
WINDOW = 512
CMP_HIDDEN = 256
Q_CHUNK = 64
FORCE_SCORE = 1e4
NSA_BRANCHES = 3
MIX_BRANCHES = 3
D_FF = -(-8 * D_MODEL // (3 * 256)) * 256
RMS_EPS = 1e-6
IN_WIDTH = SSM_WIDTH + 3 * CONV_CH + ATTN_WIDTH + 6 * KV_WIDTH + N_HEADS * NSA_BRANCHES + MIX_BRANCHES * D_MODEL

kernel_name = 'hybrid_s5_conv_nsa_gated_trunk'


def rms_norm(x, g):
    xf = x.astype(jnp.float32)
    y = xf * lax.rsqrt(jnp.mean(xf * xf, axis=-1, keepdims=True) + RMS_EPS)
    return (y * g.astype(jnp.float32)).astype(x.dtype)


def masked_softmax(s, mask):
    s = jnp.where(mask, s, -jnp.inf)
    m = jnp.max(s, axis=-1, keepdims=True)
    m = jnp.where(jnp.isfinite(m), m, 0.0)
    e = jnp.exp(s - m)
    return e / jnp.maximum(jnp.sum(e, axis=-1, keepdims=True), 1e-30)


def split_projection(z):
    sizes = ([SSM_WIDTH, CONV_CH, CONV_CH, CONV_CH, ATTN_WIDTH] + [KV_WIDTH] * 6
             + [N_HEADS * NSA_BRANCHES, MIX_BRANCHES * D_MODEL])
    return jnp.split(z, np.cumsum(sizes)[:-1].tolist(), axis=-1)


def s5_mixer(u, lam_re, lam_im, b_re, b_im, c_re, c_im, d_skip, log_dt, w_glu):
    bsz, seq, _ = u.shape
    f32 = jnp.float32
    uf = u.astype(f32).reshape(bsz, seq, SSM_GROUPS, SSM_GROUP)
    lam = lax.complex(lam_re.astype(f32), lam_im.astype(f32))
    dt = jnp.exp(log_dt.astype(f32))[:, None]
    lam_bar = jnp.exp(lam * dt)
    b = lax.complex(b_re.astype(f32), b_im.astype(f32))
    b_bar = ((lam_bar - 1.0) / lam)[..., None] * b
    c = lax.complex(c_re.astype(f32), c_im.astype(f32))
    bu = jnp.einsum('bsgh,gph->bsgp', uf.astype(jnp.complex64), b_bar)
    a = jnp.broadcast_to(lam_bar, bu.shape)

    def combine(left, right):
        a_l, b_l = left
        a_r, b_r = right
        return a_r * a_l, a_r * b_l + b_r

    _, states = lax.associative_scan(combine, (a, bu), axis=1)
    y = jnp.einsum('bsgp,ghp->bsgh', states, c).real + d_skip.astype(f32).reshape(SSM_GROUPS, SSM_GROUP) * uf
    y = jax.nn.gelu(y.reshape(bsz, seq, SSM_WIDTH)).astype(u.dtype)
    y_lin, y_gate = jnp.split(y @ w_glu, 2, axis=-1)
    return y_lin * jax.nn.sigmoid(y_gate)


def short_conv_mixer(xin, gate_b, gate_c, conv_w, w_out):
    z = gate_c * xin
    z = lax.conv_general_dilated(z, conv_w[:, None, :], window_strides=(1,), padding=[(CONV_K - 1, 0)],
                                 dimension_numbers=('NWC', 'WIO', 'NWC'), feature_group_count=CONV_CH)
    return (gate_b * z) @ w_out


def nsa_mixer(q, k_cmp, v_cmp, k_sel, v_sel, k_win, v_win, gate_logits,
              q_norm_g, k_norm_g, cmp_pe, cmp_w1, cmp_w2, w_o):
    bsz, seq, _ = q.shape
    f32 = jnp.float32
    scale = HEAD_DIM ** -0.5
    pos = jnp.arange(seq)

    def heads(t, n):
        return t.reshape(bsz, seq, n, HEAD_DIM)

    q = rms_norm(heads(q, N_HEADS), q_norm_g).reshape(bsz, seq, N_KV_HEADS, GQA, HEAD_DIM)

    n_cmp = seq // CMP_BLOCK

    def compress(t, which):
        blocks = heads(t, N_KV_HEADS).reshape(bsz, n_cmp, CMP_BLOCK, N_KV_HEADS, HEAD_DIM) + cmp_pe[which][:, None, :]
        blocks = blocks.transpose(0, 1, 3, 2, 4).reshape(bsz, n_cmp, N_KV_HEADS, CMP_BLOCK * HEAD_DIM)
        return jax.nn.gelu(blocks @ cmp_w1[which]) @ cmp_w2[which]

    kc = rms_norm(compress(k_cmp, 0), k_norm_g[0])
    vc = compress(v_cmp, 1)
    s = jnp.einsum('bqhgd,bchd->bhgqc', q, kc).astype(f32) * scale
    cmp_end = (jnp.arange(n_cmp) + 1) * CMP_BLOCK - 1
    p_cmp = masked_softmax(s, cmp_end[None, :] <= pos[:, None])
    o_cmp = jnp.einsum('bhgqc,bchd->bqhgd', p_cmp.astype(vc.dtype), vc)

    n_blocks = seq // SEL_BLOCK
    n_top = min(N_SELECT, n_blocks)
    imp = jnp.sum(p_cmp, axis=2).reshape(bsz, N_KV_HEADS, seq, n_blocks, SEL_BLOCK // CMP_BLOCK).sum(-1)
    blk = jnp.arange(n_blocks)[None, :]
    cur = (pos // SEL_BLOCK)[:, None]
    forced = (blk == 0) | (blk == cur) | (blk == cur - 1)
    visible = blk * SEL_BLOCK <= pos[:, None]
    imp = jnp.where(forced, FORCE_SCORE, jnp.where(visible, imp, -jnp.inf))
    _, sel_idx = lax.top_k(imp, n_top)

    ks_blocks = rms_norm(heads(k_sel, N_KV_HEADS), k_norm_g[1]).reshape(
        bsz, n_blocks, SEL_BLOCK, N_KV_HEADS, HEAD_DIM).transpose(0, 3, 1, 2, 4)
    vs_blocks = heads(v_sel, N_KV_HEADS).reshape(
        bsz, n_blocks, SEL_BLOCK, N_KV_HEADS, HEAD_DIM).transpose(0, 3, 1, 2, 4)
    pad = ((0, 0), (WINDOW, 0), (0, 0), (0, 0))
    kw_pad = jnp.pad(rms_norm(heads(k_win, N_KV_HEADS), k_norm_g[2]), pad)
    vw_pad = jnp.pad(heads(v_win, N_KV_HEADS), pad)
    b_ix = jnp.arange(bsz)[:, None, None, None]
    h_ix = jnp.arange(N_KV_HEADS)[None, :, None, None]
    n_sel_keys = n_top * SEL_BLOCK

    def query_chunk(c):
        start = c * Q_CHUNK
        t = start + jnp.arange(Q_CHUNK)
        qc = lax.dynamic_slice_in_dim(q, start, Q_CHUNK, axis=1)
        idx = lax.dynamic_slice_in_dim(sel_idx, start, Q_CHUNK, axis=2)
        kg = ks_blocks[b_ix, h_ix, idx].reshape(bsz, N_KV_HEADS, Q_CHUNK, n_sel_keys, HEAD_DIM)
        vg = vs_blocks[b_ix, h_ix, idx].reshape(bsz, N_KV_HEADS, Q_CHUNK, n_sel_keys, HEAD_DIM)
        key_pos = (idx[..., None] * SEL_BLOCK + jnp.arange(SEL_BLOCK)).reshape(bsz, N_KV_HEADS, Q_CHUNK, n_sel_keys)
        s_sel = jnp.einsum('bqhgd,bhqkd->bhgqk', qc, kg).astype(f32) * scale
        p_sel = masked_softmax(s_sel, (key_pos <= t[:, None])[:, :, None])
        o_sel = jnp.einsum('bhgqk,bhqkd->bqhgd', p_sel.astype(vg.dtype), vg)
        kwc = lax.dynamic_slice_in_dim(kw_pad, start, Q_CHUNK + WINDOW, axis=1)
        vwc = lax.dynamic_slice_in_dim(vw_pad, start, Q_CHUNK + WINDOW, axis=1)
        wpos = (start - WINDOW + jnp.arange(Q_CHUNK + WINDOW))[None, :]
        wmask = (wpos <= t[:, None]) & (wpos > t[:, None] - WINDOW) & (wpos >= 0)
        s_win = jnp.einsum('bqhgd,bkhd->bhgqk', qc, kwc).astype(f32) * scale
        p_win = masked_softmax(s_win, wmask)
        o_win = jnp.einsum('bhgqk,bkhd->bqhgd', p_win.astype(vwc.dtype), vwc)
        return o_sel, o_win

    o_sel, o_win = lax.map(query_chunk, jnp.arange(seq // Q_CHUNK))

    def unchunk(o):
        return jnp.moveaxis(o, 0, 1).reshape(bsz, seq, N_KV_HEADS, GQA, HEAD_DIM)

    g = jax.nn.sigmoid(gate_logits.astype(f32)).reshape(bsz, seq, N_KV_HEADS, GQA, NSA_BRANCHES).astype(q.dtype)
    o = g[..., 0:1] * o_cmp + g[..., 1:2] * unchunk(o_sel) + g[..., 2:3] * unchunk(o_win)
    return o.reshape(bsz, seq, ATTN_WIDTH) @ w_o


def setup_inputs(seed: int = 0) -> dict:
    key = jax.random.key(seed)
    ks = jax.random.split(key, 24)
    f32 = jnp.float32
    nl, G, P, H = DEPTH, SSM_GROUPS, SSM_STATE, SSM_GROUP

    def nrm(k, shape, scale):
        return scale * jax.random.normal(k, shape, f32)

    return {
        'x': nrm(ks[0], (BATCH, SEQ, D_MODEL), 1.0),
        'mix_norm_g': 1.0 + nrm(ks[1], (nl, D_MODEL), 0.02),
        'w_in': nrm(ks[2], (nl, D_MODEL, IN_WIDTH), D_MODEL ** -0.5),
        'ssm_lam_re': -0.5 + nrm(ks[3], (nl, G, P), 0.01),
        'ssm_lam_im': jnp.pi * jnp.arange(P, dtype=f32) + nrm(ks[4], (nl, G, P), 0.01),
        'ssm_b_re': nrm(ks[5], (nl, G, P, H), (2 * H) ** -0.5),
        'ssm_b_im': nrm(ks[6], (nl, G, P, H), (2 * H) ** -0.5),
        'ssm_c_re': nrm(ks[7], (nl, G, H, P), 0.5 ** 0.5),
        'ssm_c_im': nrm(ks[8], (nl, G, H, P), 0.5 ** 0.5),
        'ssm_d': nrm(ks[9], (nl, SSM_WIDTH), 1.0),
        'ssm_log_dt': jax.random.uniform(ks[10], (nl, G), f32, float(np.log(DT_MIN)), float(np.log(DT_MAX))),
        'ssm_w_glu': nrm(ks[11], (nl, SSM_WIDTH, 2 * D_MODEL), SSM_WIDTH ** -0.5),
        'conv_w': nrm(ks[12], (nl, CONV_K, CONV_CH), CONV_K ** -0.5),
        'conv_w_out': nrm(ks[13], (nl, CONV_CH, D_MODEL), CONV_CH ** -0.5),
        'q_norm_g': 1.0 + nrm(ks[14], (nl, HEAD_DIM), 0.02),
        'k_norm_g': 1.0 + nrm(ks[15], (nl, NSA_BRANCHES, HEAD_DIM), 0.02),
        'cmp_pe': nrm(ks[16], (nl, 2, CMP_BLOCK, HEAD_DIM), 0.1),
        'cmp_w1': nrm(ks[17], (nl, 2, CMP_BLOCK * HEAD_DIM, CMP_HIDDEN), (CMP_BLOCK * HEAD_DIM) ** -0.5),
        'cmp_w2': nrm(ks[18], (nl, 2, CMP_HIDDEN, HEAD_DIM), CMP_HIDDEN ** -0.5),
        'nsa_w_o': nrm(ks[19], (nl, ATTN_WIDTH, D_MODEL), ATTN_WIDTH ** -0.5),
        'w_out': nrm(ks[20], (nl, D_MODEL, D_MODEL), D_MODEL ** -0.5),
        'ffn_norm_g': 1.0 + nrm(ks[21], (nl, D_MODEL), 0.02),
        'ffn_w_gate_up': nrm(ks[22], (nl, D_MODEL, 2 * D_FF), D_MODEL ** -0.5),
        'ffn_w_down': nrm(ks[23], (nl, D_FF, D_MODEL), D_FF ** -0.5),
    }


def reference(x, mix_norm_g, w_in, ssm_lam_re, ssm_lam_im, ssm_b_re, ssm_b_im, ssm_c_re, ssm_c_im,
              ssm_d, ssm_log_dt, ssm_w_glu, conv_w, conv_w_out, q_norm_g, k_norm_g, cmp_pe, cmp_w1,
              cmp_w2, nsa_w_o, w_out, ffn_norm_g, ffn_w_gate_up, ffn_w_down):
    bsz, seq, _ = x.shape
    for i in range(DEPTH):
        h = rms_norm(x, mix_norm_g[i])
        (u, c_b, c_c, c_x, q, k_c, v_c, k_s, v_s, k_w, v_w, nsa_gates, mix_gates) = split_projection(h @ w_in[i])
        y_ssm = s5_mixer(u, ssm_lam_re[i], ssm_lam_im[i], ssm_b_re[i], ssm_b_im[i], ssm_c_re[i], ssm_c_im[i],
                         ssm_d[i], ssm_log_dt[i], ssm_w_glu[i])
        y_conv = short_conv_mixer(c_x, c_b, c_c, conv_w[i], conv_w_out[i])
        y_attn = nsa_mixer(q, k_c, v_c, k_s, v_s, k_w, v_w, nsa_gates, q_norm_g[i], k_norm_g[i],
                           cmp_pe[i], cmp_w1[i], cmp_w2[i], nsa_w_o[i])
        gates = jax.nn.sigmoid(mix_gates.astype(jnp.float32)).astype(x.dtype).reshape(bsz, seq, MIX_BRANCHES, D_MODEL)
        mixed = gates[:, :, 0] * y_ssm + gates[:, :, 1] * y_conv + gates[:, :, 2] * y_attn
        x = x + mixed @ w_out[i]
        h = rms_norm(x, ffn_norm_g[i])
        f_gate, f_up = jnp.split(h @ ffn_w_gate_up[i], 2, axis=-1)
        x = x + (jax.nn.silu(f_gate) * f_up) @ ffn_w_down[i]
    return x
```

```python
import numpy as np
from contextlib import ExitStack
import concourse.bass as bass
import concourse.mybir as mybir
from concourse.bass_utils import run_bass_kernel_spmd

ALU = mybir.AluOpType
AF = mybir.ActivationFunctionType
F32 = mybir.dt.float32
BF16 = mybir.dt.bfloat16

ENGS = ('pe', 'dve', 'act', 'pool', 'sp')
DQ = ('sp', 'pool', 'act')
NDS = 8


class Buf:
    __slots__ = ('w', 'r')

    def __init__(self):
        self.w = None
        self.r = {}


class Tl:
    def __init__(self, t, b=None):
        self.t = t
        self.b = b if b is not None else Buf()

    def __getitem__(self, k):
        return self.t[k]


def _bufs(xs):
    return [x.b if isinstance(x, Tl) else x for x in xs]


class Prog:
    def __init__(self, nc, es):
        self.nc = nc
        self.es = es
        self.streams = {e: [] for e in ENGS}
        self.cnt = {e: 0 for e in ENGS}
        self.sem = {e: es.enter_context(nc.semaphore('s_' + e)) for e in ENGS}
        self.seen = {e: {} for e in ENGS}
        self.dsem = {e: [es.enter_context(nc.semaphore('d_%s%d' % (e, i))) for i in range(NDS)] for e in DQ}
        self.dval = {e: [0] * NDS for e in DQ}
        self.drr = {e: 0 for e in DQ}
        self.uid = 0
        self.noalias = False
        self.stepcnt = {}
        self.tcache = {}
        self.reg = {}

    def tile(self, name, shape, dtype, es=None):
        self.uid += 1
        if es is not None and self.noalias:
            c = self.stepcnt.get(name, 0)
            self.stepcnt[name] = c + 1
            key = (name, c, tuple(shape), str(dtype))
            if key not in self.tcache:
                self.reg.setdefault(name, []).append('%s_%d' % (name, self.uid))
                self.tcache[key] = Tl(self.es.enter_context(
                    self.nc.sbuf_tensor('%s_%d' % (name, self.uid), list(shape), dtype)))
            return self.tcache[key]
        es = es or self.es
        self.reg.setdefault(name, []).append('%s_%d' % (name, self.uid))
        return Tl(es.enter_context(self.nc.sbuf_tensor('%s_%d' % (name, self.uid), list(shape), dtype)))

    def new_step(self):
        self.stepcnt = {}

    def ptile(self, name, shape, dtype=F32):
        return Tl(self.es.enter_context(self.nc.psum_tensor(name, list(shape), dtype)))

    def _collect(self, eng, reads, writes):
        deps = {}

        def add(h):
            if h is None:
                return
            key, sem, val = h
            if key == eng and eng == 'pe':
                return
            cur = deps.get(key)
            if cur is None or cur[1] < val:
                deps[key] = (sem, val)
        for b in reads:
            add(b.w)
        for b in writes:
            add(b.w)
            for h in b.r.values():
                add(h)
        waits = []
        seen = self.seen[eng]
        for key, (sem, val) in deps.items():
            if seen.get(key, 0) < val:
                waits.append((sem, val))
                seen[key] = val
        return waits

    def _update(self, h, reads, writes):
        for b in reads:
            cur = b.r.get(h[0])
            if cur is None or cur[2] < h[2]:
                b.r[h[0]] = h
        for b in writes:
            b.w = h
            b.r = {}

    def op(self, eng, fn, r=(), w=()):
        reads = _bufs(r)
        writes = _bufs(w)
        waits = self._collect(eng, reads, writes)
        self.cnt[eng] += 1
        h = (eng, self.sem[eng], self.cnt[eng])
        self.streams[eng].append((waits, fn, (self.sem[eng], 1)))
        self._update(h, reads, writes)
        return h

    def pe(self, fn, r=(), w=()):
        return self.op('pe', fn, r, w)

    def dve(self, fn, r=(), w=()):
        return self.op('dve', fn, r, w)

    def act(self, fn, r=(), w=()):
        return self.op('act', fn, r, w)

    def pool(self, fn, r=(), w=()):
        return self.op('pool', fn, r, w)

    def dma(self, eng, out, in_, r=(), w=(), **kw):
        reads = _bufs(r)
        writes = _bufs(w)
        waits = self._collect(eng, reads, writes)
        i = self.drr[eng]
        self.drr[eng] = (i + 1) % NDS
        sem = self.dsem[eng][i]
        prev = self.dval[eng][i]
        key = ('d', eng, i)
        if prev > 0 and self.seen[eng].get(key, 0) < prev:
            waits.append((sem, prev))
            self.seen[eng][key] = prev
        val = prev + 16
        self.dval[eng][i] = val
        h = (key, sem, val)
        self.streams[eng].append((waits, lambda e: e.dma_start(out=out, in_=in_, **kw), (sem, 16)))
        self._update(h, reads, writes)
        return h

    def barrier(self):
        ce = ('pe', 'dve', 'act', 'pool')
        for e in ce:
            waits = []
            for f in ce:
                if f == e or self.cnt[f] == 0:
                    continue
                if self.seen[e].get(f, 0) < self.cnt[f]:
                    waits.append((self.sem[f], self.cnt[f]))
                    self.seen[e][f] = self.cnt[f]
            if waits:
                self.streams[e].append((waits, None, None))

    def finish(self, eng='sp'):
        waits = []
        for q in DQ:
            for i in range(NDS):
                if self.dval[q][i] > 0:
                    waits.append((self.dsem[q][i], self.dval[q][i]))
        self.streams[eng].append((waits, None, None))

    def emit(self, seg=800):
        def replay(name, e, lo, hi):
            for waits, fn, inc in self.streams[name][lo:hi]:
                for sem, val in waits:
                    e.wait_ge(sem, val)
                if fn is not None:
                    fn(e).then_inc(inc[0], inc[1])

        with self.nc.Block() as block:
            starters = {'pe': block.tensor, 'dve': block.vector, 'act': block.scalar,
                        'pool': block.gpsimd, 'sp': block.sync}
            for name in ENGS:
                n = len(self.streams[name])
                for lo in range(0, max(n, 1), seg):
                    starters[name](lambda e, name=name, lo=lo: replay(name, e, lo, min(lo + seg, n)))


D = 1024
DFF = 2816
INW = 6424
TT = 512
C_U, C_CB, C_CC, C_CX, C_Q = 0, 512, 1024, 1536, 2048
C_KC, C_VC, C_KS, C_VS, C_KW, C_VW, C_NG, C_MG = 2560, 2688, 2816, 2944, 3072, 3200, 3328, 3352
GELU_K = 1.5957691216057308

PARAM_SHAPES = {
    'mix_norm_g': (D,), 'w_in': (D, INW), 'ssm_lam_re': (32, 64), 'ssm_lam_im': (32, 64),
    'ssm_b_re': (32, 64, 16), 'ssm_b_im': (32, 64, 16), 'ssm_c_re': (32, 16, 64), 'ssm_c_im': (32, 16, 64),
    'ssm_d': (512,), 'ssm_log_dt': (32,), 'ssm_w_glu': (512, 2048), 'conv_w': (3, 512),
    'conv_w_out': (512, D), 'q_norm_g': (64,), 'k_norm_g': (3, 64), 'cmp_pe': (2, 32, 64),
    'cmp_w1': (2, 2048, 256), 'cmp_w2': (2, 256, 64), 'nsa_w_o': (512, D), 'w_out': (D, D),
    'ffn_norm_g': (D,), 'ffn_w_gate_up': (D, 2 * DFF), 'ffn_w_down': (DFF, D),
}
BIGW = {'w_in': (D, INW), 'ssm_w_glu': (512, 2048), 'conv_w_out': (512, D), 'cmp_w1': (4096, 256),
        'cmp_w2': (512, 64), 'nsa_w_o': (512, D), 'w_out': (D, D), 'ffn_w_gate_up': (D, 2 * DFF),
        'ffn_w_down': (DFF, D)}


def build(S=4096, depth=4, flags="fcsn", noalias=False):
    nc = bass.Bass("TRN2", target_bir_lowering=False)
    NT = S // TT
    NKT = S // 128
    NBLK = S // 64
    NCMP = S // 32
    F_S, F_C, F_N, F_F = ('s' in flags), ('c' in flags), ('n' in flags), ('f' in flags)
    MIX = F_S or F_C or F_N
    x_d = nc.dram_tensor("x", [S, D], F32, kind="ExternalInput")
    out_d = nc.dram_tensor("out", [S, D], F32, kind="ExternalOutput")
    prm = {k: nc.dram_tensor(k, [depth] + list(v), F32, kind="ExternalInput") for k, v in PARAM_SHAPES.items()}
    wbd = {k: nc.dram_tensor(k + "_b", [depth] + list(v), BF16, kind="Internal") for k, v in BIGW.items()}

    with ExitStack() as es:
        P = Prog(nc, es)
        P.noalias = noalias
        nc_es = es

        def AP2(ap):
            return ap

        b_wconv = {}
        used_w = set()
        if MIX:
            used_w |= {'w_in', 'w_out'}
        if F_S:
            used_w |= {'ssm_w_glu'}
        if F_C:
            used_w |= {'conv_w_out'}
        if F_N:
            used_w |= {'cmp_w1', 'cmp_w2', 'nsa_w_o'}
        if F_F:
            used_w |= {'ffn_w_gate_up', 'ffn_w_down'}
        for l in range(depth):
            for k, (rows, cols) in BIGW.items():
                if k not in used_w:
                    continue
                b = Buf()
                b_wconv[(k, l)] = b
                src = prm[k][l]
                if k in ('cmp_w1', 'cmp_w2'):
                    src = src.rearrange("a r c -> (a r) c")
                dst = wbd[k][l]
                rstep = max(16, min(rows, ((1 << 20) // cols) // 16 * 16))
                hs = []
                for r0 in range(0, rows, rstep):
                    r1 = min(rows, r0 + rstep)
                    hs.append(P.dma('pool', dst[r0:r1, :], src[r0:r1, :]))
                b.w = None
                b.r = {}
                b_wconv[(k, l)] = [Buf() for _ in hs]
                for bb, h in zip(b_wconv[(k, l)], hs):
                    bb.w = h

        ident_f = P.tile("ident_f", [128, 128], F32)
        ident_b = P.tile("ident_b", [128, 128], BF16)
        ones_f = P.tile("ones_f", [128, 128], F32)
        P.pool(lambda e: e.memset(ones_f[:, :], 1.0), w=[ones_f])
        P.pool(lambda e: e.affine_select(out=ident_f[:, :], in_=ones_f[:, :], pattern=[[-1, 128]], compare_op=ALU.is_equal,
                                         fill=0.0, base=0, channel_multiplier=1), r=[ones_f], w=[ident_f])
        P.dve(lambda e: e.tensor_copy(out=ident_b[:, :], in_=ident_f[:, :]), r=[ident_f], w=[ident_b])

        PB = [P.ptile("pb%d" % i, [128, 512]) for i in range(7)]
        psT = P.ptile("pbT", [128, 1024], BF16)
        rr = {'A': 0, 'S': 0}

        def psA():
            rr['A'] = (rr['A'] + 1) % rr.get('An', 3)
            return PB[rr['A']]

        def psS():
            rr['S'] = (rr['S'] + 1) % 2
            return PB[3 + rr['S']]

        NWB = 4
        wbufs = [P.tile("wbuf%d" % i, [128, 4096], BF16) for i in range(NWB)]
        wrr = [0]

        def wload(name, l, r0, kc, c0, ncols):
            wrr[0] = (wrr[0] + 1) % NWB
            wt = wbufs[wrr[0]]
            view = wt[:, 0:kc * ncols].rearrange("p (k c) -> p k c", k=kc)
            src = wbd[name][l][r0:r0 + kc * 128, c0:c0 + ncols].rearrange("(k p) c -> p k c", p=128)
            P.dma('sp', view, src, r=b_wconv[(name, l)], w=[wt])
            return wt, view

        xt = P.tile("xt", [128, 4, D], F32)
        xts = [Tl(xt.t, Buf()) for _ in range(4)]
        hT = P.tile("hT", [128, 8, TT], BF16)
        mixed = P.tile("mixed", [128, 8, TT], F32) if MIX else None
        b_xd = [[Buf() for _ in range(4)] for _ in range(NT)]
        gfm = P.tile("gfm", [128, 2, 8], F32)
        ss = P.tile("ss", [128, 4], F32)
        rstd = P.tile("rstd", [128, 4], F32)

        def rmsnorm_T(which, sc):
            junk = P.tile("junk", [128, D], BF16, sc)
            xn = P.tile("xn", [128, D], BF16, sc)
            for s in range(4):
                P.act(lambda e, s=s: e.activation(out=junk[:, :], in_=xt[:, s, :], func=AF.Square, accum_out=ss[:, s:s + 1]),
                      r=[xts[s]], w=[junk, ss])
            P.act(lambda e: e.activation(out=rstd[:, :], in_=ss[:, :], func=AF.Sqrt, scale=1.0 / D, bias=epsb[:, 0:1]),
                  r=[ss, epsb], w=[rstd])
            P.dve(lambda e: e.reciprocal(out=rstd[:, :], in_=rstd[:, :]), r=[rstd], w=[rstd])
            for s in range(4):
                P.dve(lambda e, s=s: e.tensor_scalar(out=xn[:, :], in0=xt[:, s, :], scalar1=rstd[:, s:s + 1], scalar2=None,
                                                     op0=ALU.mult), r=[xts[s], rstd], w=[xn])
                for k in range(8):
                    P.pe(lambda e, k=k: e.transpose(out=psT[:, k * 128:(k + 1) * 128], in_=xn[:, k * 128:(k + 1) * 128],
                                                    identity=ident_b[:, :]), r=[xn, ident_b], w=[psT])
                P.dve(lambda e, s=s: e.tensor_tensor(
                    out=hT[:, :, s * 128:(s + 1) * 128], in0=psT[:, :].rearrange("p (k c) -> p k c", k=8),
                    in1=gfm[:, which, :].unsqueeze(2).to_broadcast([128, 8, 128]), op=ALU.mult),
                    r=[psT, gfm], w=[hT])

        def fm_mm(pt, view, cols, rhs_fn, kc, M=128, rd=()):
            for k in range(kc):
                rhs_ = rhs_fn(k)
                lt_ = view[:, k, cols[0]:cols[1]]
                P.pe(lambda e, k=k, rhs_=rhs_, lt_=lt_: e.matmul(pt[0:M, :], lhsT=lt_, rhs=rhs_,
                                                                 start=(k == 0), stop=(k == kc - 1)), r=list(rd), w=[pt])

        epsb = P.tile("epsb", [128, 2], F32)
        P.pool(lambda e: e.memset(epsb[:, :], 1e-6), w=[epsb])

        first_mix = [True] * 8
        sgp = {}
        glp = {}
        FR = 256
        if F_S:
            tabre = P.tile("tabre", [128, 16, FR], F32)
            tabim = P.tile("tabim", [128, 16, FR], F32)
            BT = P.tile("BT", [128, 16, 2, 128], BF16)
            CT = P.tile("CT", [128, 16, 2, 128], BF16)
            DT = P.tile("DT", [128, 4, 128], BF16)
            CTn = P.tile("CTn", [128, 16, 128], BF16)
            rdec = P.tile("rdec", [128, 16], F32)
            car = P.tile("car", [128, 16, 2], F32)
            halfpi = P.tile("halfpi", [128, 1], F32)
            lre = P.tile("lre", [128, 16], F32)
            lim = P.tile("lim", [128, 16], F32)
            ldt = P.tile("ldt", [128, 16], F32)
            Bsb = P.tile("Bsb", [128, 2, 16, 16], F32)
            Csb = P.tile("Csb", [128, 2, 2, 128], F32)
            dsb = P.tile("dsb", [128, 4], F32)
            P.pool(lambda e: e.memset(halfpi[:, :], float(np.pi / 2)), w=[halfpi])

        if F_N:
            WV = 65 + NBLK
            ksT = P.tile("ksT", [128, S], BF16)
            kwT = P.tile("kwT", [128, S], BF16)
            vs = P.tile("vs", [128, NKT, 2, 65], BF16)
            vw = P.tile("vw", [128, NKT, 2, 65], BF16)
            kcT = P.tile("kcT", [128, 128], BF16)
            vcT = P.tile("vcT", [128, 128], BF16)
            vcaug = P.tile("vcaug", [128, 2, WV], BF16)
            Em = P.tile("Em", [NBLK, NKT, 128], BF16)
            TA = P.tile("TA", [128, 2 * NBLK], F32)
            TB = P.tile("TB", [128, 2 * NBLK], F32)
            tri_le = P.tile("tri_le", [128, 128], BF16)
            tri_gt = P.tile("tri_gt", [128, 128], BF16)
            bd1 = P.tile("bd1", [128, 128], F32)
            gq = P.tile("gq", [128, 1], F32)
            gk = P.tile("gk", [128, 3], F32)
            peT = P.tile("peT", [128, 2, 32], F32)
            w2sb = P.tile("w2sb", [128, 2, 2, 64], BF16)
            mT = P.tile("mT", [64, 2, 128], BF16)
            P.pool(lambda e: e.memset(vs[:, :, :, :], 1.0), w=[vs])
            P.pool(lambda e: e.memset(vw[:, :, :, :], 1.0), w=[vw])
            P.pool(lambda e: e.memset(vcaug[:, :, :], 1.0), w=[vcaug])
            for h_ in range(2):
                P.pool(lambda e, h_=h_: e.affine_select(out=vcaug[:, h_, 65:WV], in_=vcaug[:, h_, 65:WV], pattern=[[-2, NBLK]],
                                                compare_op=ALU.is_ge, fill=0.0, base=0, channel_multiplier=1), r=[vcaug], w=[vcaug])
                P.pool(lambda e, h_=h_: e.affine_select(out=vcaug[:, h_, 65:WV], in_=vcaug[:, h_, 65:WV], pattern=[[2, NBLK]],
                                                compare_op=ALU.is_ge, fill=0.0, base=1, channel_multiplier=-1), r=[vcaug], w=[vcaug])
            P.pool(lambda e: e.memset(Em[:, :, :], 1.0), w=[Em])
            P.pool(lambda e: e.affine_select(out=Em[:, :, :], in_=Em[:, :, :], pattern=[[128, NKT], [1, 128]], compare_op=ALU.is_ge,
                                             fill=0.0, base=0, channel_multiplier=-64), r=[Em], w=[Em])
            P.pool(lambda e: e.affine_select(out=Em[:, :, :], in_=Em[:, :, :], pattern=[[-128, NKT], [-1, 128]], compare_op=ALU.is_ge,
                                             fill=0.0, base=63, channel_multiplier=64), r=[Em], w=[Em])
            cc_ = NBLK
            P.pool(lambda e: e.memset(TA[:, :], 1.0), w=[TA])
            P.pool(lambda e: e.memset(TA[0:64, cc_ - 1:], 0.0), w=[TA])
            P.pool(lambda e: e.memset(TA[64:128, cc_:], 0.0), w=[TA])
            P.pool(lambda e: e.memset(TB[:, :], 0.0), w=[TB])
            P.pool(lambda e: e.memset(TB[0:64, cc_ + 1:], -1e30), w=[TB])
            P.pool(lambda e: e.memset(TB[0:64, cc_ - 1:cc_ + 1], 1e4), w=[TB])
            P.pool(lambda e: e.memset(TB[64:128, cc_ + 2:], -1e30), w=[TB])
            P.pool(lambda e: e.memset(TB[64:128, cc_:cc_ + 2], 1e4), w=[TB])
            onesb = P.tile("onesb", [128, 128], BF16)
            P.pool(lambda e: e.memset(onesb[:, :], 1.0), w=[onesb])
            P.pool(lambda e: e.affine_select(out=tri_le[:, :], in_=onesb[:, :], pattern=[[1, 128]], compare_op=ALU.is_ge,
                                             fill=0.0, base=0, channel_multiplier=-1), r=[onesb], w=[tri_le])
            P.pool(lambda e: e.affine_select(out=tri_gt[:, :], in_=onesb[:, :], pattern=[[-1, 128]], compare_op=ALU.is_gt,
                                             fill=0.0, base=0, channel_multiplier=1), r=[onesb], w=[tri_gt])
            P.pool(lambda e: e.memset(bd1[:, :], 0.0), w=[bd1])
            P.pool(lambda e: e.memset(bd1[0:64, 0:64], 1.0), w=[bd1])
            P.pool(lambda e: e.memset(bd1[64:128, 64:128], 1.0), w=[bd1])

        def nsa_setup(l):
            for hh in range(2):
                rows = slice(hh * 64, hh * 64 + 64)
                P.dma('sp', gq[rows, :], prm['q_norm_g'][l].rearrange("(d o) -> d o", o=1), w=[gq])
                P.dma('sp', gk[rows, :], prm['k_norm_g'][l].rearrange("b d -> d b"), w=[gk], allow_slow_non_contiguous=True)
                for w_ in range(2):
                    P.dma('sp', peT[rows, w_, :], prm['cmp_pe'][l][w_].rearrange("l d -> d l"), w=[peT],
                          allow_slow_non_contiguous=True)
            P.dve(lambda e: e.tensor_scalar(out=gq[:, :], in0=gq[:, :], scalar1=0.125, scalar2=None, op0=ALU.mult), r=[gq], w=[gq])
            for w_ in range(2):
                P.dma('sp', w2sb[:, w_, :, :], wbd['cmp_w2'][l][w_ * 256:(w_ + 1) * 256, :].rearrange("(j p) d -> p j d", p=128),
                      r=b_wconv[('cmp_w2', l)], w=[w2sb])
            P.pool(lambda e: e.memset(kcT[:, :], 0.0), w=[kcT])
            P.pool(lambda e: e.memset(vcT[:, :], 0.0), w=[vcT])

        def s5_setup(l):
            P.new_step()
            with ExitStack() as sc:
                def sm(name, shape=(128, 16)):
                    return P.tile(name, list(shape), F32, sc)
                P.dma('sp', lre[:, :], bass.AP(prm['ssm_lam_re'], l * 2048, [[1, 128], [128, 16]]), w=[lre],
                      allow_slow_non_contiguous=True)
                P.dma('sp', lim[:, :], bass.AP(prm['ssm_lam_im'], l * 2048, [[1, 128], [128, 16]]), w=[lim],
                      allow_slow_non_contiguous=True)
                for gl in range(2):
                    P.dma('sp', ldt[gl * 64:(gl + 1) * 64, :], bass.AP(prm['ssm_log_dt'], l * 32 + gl, [[0, 64], [2, 16]]),
                          w=[ldt], allow_slow_non_contiguous=True)
                dt_ = sm("dt")
                P.act(lambda e: e.activation(out=dt_[:, :], in_=ldt[:, :], func=AF.Exp), r=[ldt], w=[dt_])
                are, th = sm("are"), sm("th")
                P.dve(lambda e: e.tensor_tensor(out=are[:, :], in0=lre[:, :], in1=dt_[:, :], op=ALU.mult), r=[lre, dt_], w=[are])
                P.dve(lambda e: e.tensor_tensor(out=th[:, :], in0=lim[:, :], in1=dt_[:, :], op=ALU.mult), r=[lim, dt_], w=[th])
                P.act(lambda e: e.activation(out=rdec[:, :], in_=are[:, :], func=AF.Exp), r=[are], w=[rdec])
                c_, s_ = sm("c"), sm("s")
                P.act(lambda e: e.activation(out=s_[:, :], in_=th[:, :], func=AF.Sin, scale=1.0 / 32), r=[th], w=[s_])
                P.act(lambda e: e.activation(out=c_[:, :], in_=th[:, :], func=AF.Sin, scale=1.0 / 32, bias=halfpi[:, 0:1]),
                      r=[th, halfpi], w=[c_])
                t1, t2, t3 = sm("t1"), sm("t2"), sm("t3")
                for _ in range(5):
                    P.dve(lambda e: e.tensor_tensor(out=t1[:, :], in0=c_[:, :], in1=c_[:, :], op=ALU.mult), r=[c_], w=[t1])
                    P.dve(lambda e: e.tensor_tensor(out=t2[:, :], in0=s_[:, :], in1=s_[:, :], op=ALU.mult), r=[s_], w=[t2])
                    P.dve(lambda e: e.tensor_tensor(out=t3[:, :], in0=c_[:, :], in1=s_[:, :], op=ALU.mult), r=[c_, s_], w=[t3])
                    P.dve(lambda e: e.tensor_tensor(out=c_[:, :], in0=t1[:, :], in1=t2[:, :], op=ALU.subtract), r=[t1, t2], w=[c_])
                    P.dve(lambda e: e.tensor_scalar(out=s_[:, :], in0=t3[:, :], scalar1=2.0, scalar2=None, op0=ALU.mult),
                          r=[t3], w=[s_])
                nre, nim, den, cre, cim = sm("nre"), sm("nim"), sm("den"), sm("cre"), sm("cim")
                P.dve(lambda e: e.tensor_tensor(out=nre[:, :], in0=rdec[:, :], in1=c_[:, :], op=ALU.mult), r=[rdec, c_], w=[nre])
                P.dve(lambda e: e.tensor_scalar(out=nre[:, :], in0=nre[:, :], scalar1=-1.0, scalar2=None, op0=ALU.add),
                      r=[nre], w=[nre])
                P.dve(lambda e: e.tensor_tensor(out=nim[:, :], in0=rdec[:, :], in1=s_[:, :], op=ALU.mult), r=[rdec, s_], w=[nim])
                P.dve(lambda e: e.tensor_tensor(out=t1[:, :], in0=lre[:, :], in1=lre[:, :], op=ALU.mult), r=[lre], w=[t1])
                P.dve(lambda e: e.tensor_tensor(out=t2[:, :], in0=lim[:, :], in1=lim[:, :], op=ALU.mult), r=[lim], w=[t2])
                P.dve(lambda e: e.tensor_tensor(out=den[:, :], in0=t1[:, :], in1=t2[:, :], op=ALU.add), r=[t1, t2], w=[den])
                P.dve(lambda e: e.reciprocal(out=den[:, :], in_=den[:, :]), r=[den], w=[den])
                P.dve(lambda e: e.tensor_tensor(out=t1[:, :], in0=nre[:, :], in1=lre[:, :], op=ALU.mult), r=[nre, lre], w=[t1])
                P.dve(lambda e: e.tensor_tensor(out=t2[:, :], in0=nim[:, :], in1=lim[:, :], op=ALU.mult), r=[nim, lim], w=[t2])
                P.dve(lambda e: e.tensor_tensor(out=t1[:, :], in0=t1[:, :], in1=t2[:, :], op=ALU.add), r=[t1, t2], w=[t1])
                P.dve(lambda e: e.tensor_tensor(out=cre[:, :], in0=t1[:, :], in1=den[:, :], op=ALU.mult), r=[t1, den], w=[cre])
                P.dve(lambda e: e.tensor_tensor(out=t1[:, :], in0=nim[:, :], in1=lre[:, :], op=ALU.mult), r=[nim, lre], w=[t1])
                P.dve(lambda e: e.tensor_tensor(out=t2[:, :], in0=nre[:, :], in1=lim[:, :], op=ALU.mult), r=[nre, lim], w=[t2])
                P.dve(lambda e: e.tensor_tensor(out=t1[:, :], in0=t1[:, :], in1=t2[:, :], op=ALU.subtract), r=[t1, t2], w=[t1])
                P.dve(lambda e: e.tensor_tensor(out=cim[:, :], in0=t1[:, :], in1=den[:, :], op=ALU.mult), r=[t1, den], w=[cim])
                P.dve(lambda e: e.tensor_copy(out=tabre[:, :, 0:1], in_=c_[:, :].unsqueeze(2)), r=[c_], w=[tabre])
                P.dve(lambda e: e.tensor_scalar(out=tabim[:, :, 0:1], in0=s_[:, :].unsqueeze(2), scalar1=-1.0, scalar2=None,
                                                op0=ALU.mult), r=[s_], w=[tabim])
                with ExitStack() as sc2:
                    tA = P.tile("tA", [128, 16, FR // 2], F32, sc2)
                    tB = P.tile("tB", [128, 16, FR // 2], F32, sc2)
                    tCf = P.tile("tC", [128, 16, FR // 2], F32, sc2)
                    n = 1
                    while n < FR:
                        def bc(t, n=n):
                            return t[:, :, n - 1:n].to_broadcast([128, 16, n])
                        P.dve(lambda e, n=n, bc=bc: e.tensor_tensor(out=tA[:, :, 0:n], in0=tabre[:, :, 0:n], in1=bc(tabre), op=ALU.mult),
                              r=[tabre], w=[tA])
                        P.dve(lambda e, n=n, bc=bc: e.tensor_tensor(out=tB[:, :, 0:n], in0=tabim[:, :, 0:n], in1=bc(tabim), op=ALU.mult),
                              r=[tabim], w=[tB])
                        P.dve(lambda e, n=n: e.tensor_tensor(out=tCf[:, :, 0:n], in0=tA[:, :, 0:n], in1=tB[:, :, 0:n], op=ALU.subtract),
                              r=[tA, tB], w=[tCf])
                        P.dve(lambda e, n=n, bc=bc: e.tensor_tensor(out=tA[:, :, 0:n], in0=tabre[:, :, 0:n], in1=bc(tabim), op=ALU.mult),
                              r=[tabre, tabim], w=[tA])
                        P.dve(lambda e, n=n, bc=bc: e.tensor_tensor(out=tB[:, :, 0:n], in0=tabim[:, :, 0:n], in1=bc(tabre), op=ALU.mult),
                              r=[tabre, tabim], w=[tB])
                        P.dve(lambda e, n=n: e.tensor_tensor(out=tabim[:, :, n:2 * n], in0=tA[:, :, 0:n], in1=tB[:, :, 0:n], op=ALU.add),
                              r=[tA, tB], w=[tabim])
                        P.dve(lambda e, n=n: e.tensor_copy(out=tabre[:, :, n:2 * n], in_=tCf[:, :, 0:n]), r=[tCf], w=[tabre])
                        n *= 2
                P.barrier()
                with ExitStack() as sc3:
                    pass
                    for ri, nm in enumerate(('ssm_b_re', 'ssm_b_im')):
                        P.dma('sp', Bsb[:, ri, :, :], bass.AP(prm[nm], l * 32768, [[16, 128], [2048, 16], [1, 16]]), w=[Bsb])
                    Z = P.tile("Z", [128, 16, 128], F32, sc3)
                    P.pool(lambda e: e.memset(Z[:, :, :], 0.0), w=[Z])
                    Zv = Z[:, :, :].rearrange("p (a b) c -> p a b c", b=4)
                    for ri in range(2):
                        Bv = Bsb[:, ri, :, :].rearrange("p (a b) h -> p a b h", b=4)
                        for b_ in range(4):
                            for gl in range(2):
                                c0 = 32 * b_ + 16 * gl
                                P.dve(lambda e, b_=b_, gl=gl, c0=c0, Bv=Bv: e.tensor_copy(
                                    out=Zv[gl * 64:(gl + 1) * 64, :, b_, c0:c0 + 16], in_=Bv[gl * 64:(gl + 1) * 64, :, b_, :]),
                                    r=[Bsb], w=[Z])
                        for j in range(16):
                            pt = psA()
                            P.pe(lambda e, j=j, pt=pt: e.transpose(out=pt[:, 0:128], in_=Z[:, j, :], identity=ident_f[:, :]),
                                 r=[Z, ident_f], w=[pt])
                            P.act(lambda e, j=j, pt=pt, ri=ri: e.activation(out=BT[:, j, ri, :], in_=pt[:, 0:128], func=AF.Copy),
                                  r=[pt], w=[BT])
                    pass
                    for ri, nm in enumerate(('ssm_c_re', 'ssm_c_im')):
                        for j in range(16):
                            P.dma('sp', Csb[(j % 8) * 16:(j % 8) * 16 + 16, j // 8, ri, :].rearrange("p (g q) -> p g q", g=2),
                                  bass.AP(prm[nm], l * 32768 + j * 2048, [[64, 16], [1024, 2], [1, 64]]), w=[Csb])
                    Cr = P.tile("Cr", [128, 2, 16, 16], F32, sc3)
                    for ri in range(2):
                        for a in range(2):
                            pt = psA()
                            P.pe(lambda e, a=a, ri=ri, pt=pt: e.transpose(out=pt[:, 0:128], in_=Csb[:, a, ri, :], identity=ident_f[:, :]),
                                 r=[Csb, ident_f], w=[pt])
                            P.act(lambda e, a=a, ri=ri, pt=pt: e.activation(
                                out=Cr[:, ri, a * 8:(a + 1) * 8, :], in_=pt[:, 0:128].rearrange("p (j h) -> p j h", h=16), func=AF.Copy),
                                r=[pt], w=[Cr])
                    Cp = P.tile("Cp", [128, 2, 16, 16], F32, sc3)
                    u1 = P.tile("u1", [128, 16, 16], F32, sc3)
                    u2 = P.tile("u2", [128, 16, 16], F32, sc3)

                    def cb(t):
                        return t[:, :].unsqueeze(2).to_broadcast([128, 16, 16])
                    P.dve(lambda e: e.tensor_tensor(out=u1[:, :, :], in0=Cr[:, 0, :, :], in1=cb(cre), op=ALU.mult), r=[Cr, cre], w=[u1])
                    P.dve(lambda e: e.tensor_tensor(out=u2[:, :, :], in0=Cr[:, 1, :, :], in1=cb(cim), op=ALU.mult), r=[Cr, cim], w=[u2])
                    P.dve(lambda e: e.tensor_tensor(out=Cp[:, 0, :, :], in0=u1[:, :, :], in1=u2[:, :, :], op=ALU.subtract),
                          r=[u1, u2], w=[Cp])
                    P.dve(lambda e: e.tensor_tensor(out=u1[:, :, :], in0=Cr[:, 0, :, :], in1=cb(cim), op=ALU.mult), r=[Cr, cim], w=[u1])
                    P.dve(lambda e: e.tensor_tensor(out=u2[:, :, :], in0=Cr[:, 1, :, :], in1=cb(cre), op=ALU.mult), r=[Cr, cre], w=[u2])
                    P.dve(lambda e: e.scalar_tensor_tensor(out=Cp[:, 1, :, :], in0=u1[:, :, :], scalar=-1.0, in1=u2[:, :, :],
                                                           op0=ALU.mult, op1=ALU.subtract), r=[u1, u2], w=[Cp])
                    P.pool(lambda e: e.memset(CT[:, :, :, :], 0.0), w=[CT])
                    CTv = CT[:, :, :, :].rearrange("p (a b) r c -> p a b r c", b=4)
                    for ri in range(2):
                        Cv = Cp[:, ri, :, :].rearrange("p (a b) h -> p a b h", b=4)
                        for b_ in range(4):
                            for gl in range(2):
                                c0 = 32 * b_ + 16 * gl
                                P.dve(lambda e, b_=b_, gl=gl, c0=c0, Cv=Cv, ri=ri: e.tensor_copy(
                                    out=CTv[gl * 64:(gl + 1) * 64, :, b_, ri, c0:c0 + 16], in_=Cv[gl * 64:(gl + 1) * 64, :, b_, :]),
                                    r=[Cp], w=[CT])
                    P.dve(lambda e: e.tensor_scalar(out=CTn[:, :, :], in0=CT[:, :, 1, :], scalar1=-1.0, scalar2=None, op0=ALU.mult),
                          r=[CT], w=[CTn])
                    P.dma('sp', dsb[:, :], prm['ssm_d'][l].rearrange("(k p) -> p k", p=128), w=[dsb], allow_slow_non_contiguous=True)
                    for k in range(4):
                        P.dve(lambda e, k=k: e.tensor_scalar(out=DT[:, k, :], in0=ident_f[:, :], scalar1=dsb[:, k:k + 1], scalar2=None,
                                                             op0=ALU.mult), r=[ident_f, dsb], w=[DT])
                P.barrier()
                P.pool(lambda e: e.memset(car[:, :, :], 0.0), w=[car])
            P.barrier()

        def gelu_to(out_ap, out_tl, src_ps, src_tl, sc, shape, temps=None):
            kk_ = (id(sc), tuple(shape))
            if temps is None and glp.get('k') != kk_:
                glp.clear()
                glp['k'] = kk_
                glp['t'] = [(P.tile("gx", shape, F32, sc), P.tile("gx2", shape, F32, sc)) for _ in range(1)]
                glp['i'] = 0
            xs, x2 = temps if temps is not None else glp['t'][glp['i']]
            pp = shape[0]
            P.act(lambda e: e.activation(out=xs[:, :], in_=src_ps, func=AF.Copy), r=[src_tl], w=[xs])
            P.act(lambda e: e.activation(out=x2[:, :], in_=src_ps, func=AF.Square), r=[src_tl], w=[x2])
            P.dve(lambda e: e.tensor_scalar(out=x2[:, :], in0=x2[:, :], scalar1=0.044715, scalar2=1.0, op0=ALU.mult, op1=ALU.add),
                  r=[x2], w=[x2])
            P.dve(lambda e: e.tensor_tensor(out=x2[:, :], in0=x2[:, :], in1=xs[:, :], op=ALU.mult), r=[x2, xs], w=[x2])
            P.act(lambda e: e.activation(out=x2[:, :], in_=x2[:, :], func=AF.Sigmoid, scale=GELU_K), r=[x2], w=[x2])
            P.dve(lambda e: e.tensor_tensor(out=out_ap, in0=x2[:, :], in1=xs[:, :], op=ALU.mult), r=[x2, xs], w=[out_tl])
        if F_C:
            cw = P.tile("cw", [128, 3, 4], F32)
            halo = P.tile("halo", [128, 4, 2], F32)

        def acc_mixed(k, src_ap, src_tl, sc, l, gate_view_fn, branch):
            wt, gv = gate_view_fn(k)
            pg = psA()
            fm_mm(pg, gv, ((k % 4) * 128, (k % 4) * 128 + 128), lambda kk: hT[:, kk, :], 8, rd=[wt, hT])
            kk_ = id(sc)
            if sgp.get('k') != kk_:
                sgp.clear()
                sgp['k'] = kk_
                sgp['t'] = [P.tile("sg", [128, TT], F32, sc) for _ in range(2)]
                sgp['i'] = 0
            sgp['i'] ^= 1
            sg = sgp['t'][sgp['i']]
            P.act(lambda e: e.activation(out=sg[:, :], in_=pg[:, :], func=AF.Sigmoid), r=[pg], w=[sg])
            if first_mix[k]:
                P.dve(lambda e: e.tensor_tensor(out=mixed[:, k, :], in0=src_ap, in1=sg[:, :], op=ALU.mult),
                      r=[src_tl, sg], w=[mixed])
                first_mix[k] = False
            else:
                P.dve(lambda e: e.tensor_tensor(out=sg[:, :], in0=src_ap, in1=sg[:, :], op=ALU.mult),
                      r=[src_tl, sg], w=[sg])
                P.pool(lambda e: e.tensor_tensor(out=mixed[:, k, :], in0=mixed[:, k, :], in1=sg[:, :], op=ALU.add),
                       r=[mixed, sg], w=[mixed])

        for l in range(depth):
            P.dma('sp', gfm[:, 0, :], prm['mix_norm_g'][l].rearrange("(k p) -> p k", p=128), w=[gfm],
                  allow_slow_non_contiguous=True)
            P.dma('sp', gfm[:, 1, :], prm['ffn_norm_g'][l].rearrange("(k p) -> p k", p=128), w=[gfm],
                  allow_slow_non_contiguous=True)
            if F_S:
                s5_setup(l)
            if F_N:
                nsa_setup(l)
            if F_C:
                for t_ in range(3):
                    P.dma('sp', cw[:, t_, :], prm['conv_w'][l][t_].rearrange("(k p) -> p k", p=128), w=[cw],
                          allow_slow_non_contiguous=True)
                P.pool(lambda e: e.memset(halo[:, :, :], 0.0), w=[halo])
            def step(l, i):
                P.new_step()
                t0 = i * TT
                src = x_d if l == 0 else out_d
                for s_ in range(4):
                    P.dma('sp', xt[:, s_, :], src[t0 + s_ * 128:t0 + (s_ + 1) * 128, :], r=[b_xd[i][s_]], w=[xts[s_]])
                if MIX:
                    for k in range(8):
                        first_mix[k] = True
                    with ExitStack() as sc:
                        rmsnorm_T(0, sc)
                    P.barrier()
                    gate_cache = {}

                    def gate_view(branch):
                        def f(k):
                            key = (branch, k // 4)
                            if key not in gate_cache:
                                gate_cache.clear()
                                gate_cache[key] = wload('w_in', l, 0, 8, C_MG + branch * D + (k // 4) * 512, 512)
                            return gate_cache[key]
                        return f
                    if F_S:
                        with ExitStack() as sc:
                            uT = P.tile("uT", [128, 4, TT], BF16, sc)
                            yact = P.tile("yact", [128, 4, TT], BF16, sc)
                            wt, wv = wload('w_in', l, 0, 8, C_U, 512)
                            for k in range(4):
                                pt = psA()
                                fm_mm(pt, wv, (k * 128, k * 128 + 128), lambda kk: hT[:, kk, :], 8, rd=[wt, hT])
                                P.act(lambda e, k=k, pt=pt: e.activation(out=uT[:, k, :], in_=pt[:, :], func=AF.Copy),
                                      r=[pt], w=[uT])
                            with ExitStack() as scw:
                                WS = []
                                for _ in range(2):
                                    WS.append(dict(
                                        t=[P.tile("s5t", [128, FR], F32, scw) for _ in range(4)],
                                        wre=P.tile("wre", [128, FR], F32, scw), wim=P.tile("wim", [128, FR], F32, scw),
                                        gre=P.tile("gre", [128, FR], F32, scw), gim=P.tile("gim", [128, FR], F32, scw),
                                        b=[P.tile("s5b", [128, FR], BF16, scw) for _ in range(4)], cx=P.tile("cx", [128, 2], F32, scw)))
                                carj = [Tl(car.t, Buf()) for _ in range(16)]
                                for cj in carj:
                                    cj.b.w = car.b.w

                                def s5_chain(j, k, f, ws, yacc, first, bp):
                                    t1, t2, t3, t4 = ws['t']
                                    wre, wim, gre, gim = ws['wre'], ws['wim'], ws['gre'], ws['gim']
                                    b1, b2, b3, b4 = ws['b']
                                    cx = ws['cx']
                                    cj = carj[j]
                                    c0, c1 = f * FR, (f + 1) * FR
                                    bur, bui = bp
                                    P.pe(lambda e: e.matmul(bur[:, 0:FR], lhsT=BT[:, j, 0, :], rhs=uT[:, k, c0:c1], start=True, stop=True),
                                         r=[BT, uT], w=[bur])
                                    P.pe(lambda e: e.matmul(bui[:, 0:FR], lhsT=BT[:, j, 1, :], rhs=uT[:, k, c0:c1], start=True, stop=True),
                                         r=[BT, uT], w=[bui])
                                    yield
                                    tre = tabre[:, j, :]
                                    tim = tabim[:, j, :]
                                    P.dve(lambda e: e.tensor_tensor(out=t1[:, :], in0=bur[:, 0:FR], in1=tre, op=ALU.mult), r=[bur, tabre], w=[t1])
                                    P.dve(lambda e: e.tensor_tensor(out=t2[:, :], in0=bui[:, 0:FR], in1=tim, op=ALU.mult), r=[bui, tabim], w=[t2])
                                    yield
                                    P.pool(lambda e: e.tensor_tensor(out=wre[:, :], in0=t1[:, :], in1=t2[:, :], op=ALU.subtract), r=[t1, t2], w=[wre])
                                    P.dve(lambda e: e.tensor_tensor(out=t3[:, :], in0=bui[:, 0:FR], in1=tre, op=ALU.mult), r=[bui, tabre], w=[t3])
                                    P.dve(lambda e: e.tensor_tensor(out=t4[:, :], in0=bur[:, 0:FR], in1=tim, op=ALU.mult), r=[bur, tabim], w=[t4])
                                    yield
                                    P.pool(lambda e: e.tensor_tensor(out=wim[:, :], in0=t3[:, :], in1=t4[:, :], op=ALU.add), r=[t3, t4], w=[wim])
                                    dec = rdec[:, j:j + 1].to_broadcast([128, FR])
                                    P.dve(lambda e: e.tensor_tensor_scan(out=gre[:, :], data0=dec, data1=wre[:, :], initial=car[:, j, 0:1],
                                                                         op0=ALU.mult, op1=ALU.add), r=[rdec, wre, cj], w=[gre])
                                    yield
                                    P.dve(lambda e: e.tensor_tensor_scan(out=gim[:, :], data0=dec, data1=wim[:, :], initial=car[:, j, 1:2],
                                                                         op0=ALU.mult, op1=ALU.add), r=[rdec, wim, cj], w=[gim])
                                    P.dve(lambda e: e.tensor_tensor(out=b1[:, :], in0=gre[:, :], in1=tre, op=ALU.mult), r=[gre, tabre], w=[b1])
                                    yield
                                    P.dve(lambda e: e.tensor_tensor(out=b2[:, :], in0=gim[:, :], in1=tim, op=ALU.mult), r=[gim, tabim], w=[b2])
                                    P.dve(lambda e: e.tensor_tensor(out=b3[:, :], in0=gim[:, :], in1=tre, op=ALU.mult), r=[gim, tabre], w=[b3])
                                    P.pool(lambda e: e.tensor_tensor(out=cx[:, 0:1], in0=gim[:, FR - 1:FR], in1=tabim[:, j, FR - 1:FR], op=ALU.mult),
                                           r=[gim, tabim], w=[cx])
                                    yield
                                    P.dve(lambda e: e.tensor_tensor(out=b4[:, :], in0=gre[:, :], in1=tim, op=ALU.mult), r=[gre, tabim], w=[b4])
                                    P.pool(lambda e: e.tensor_tensor(out=cx[:, 1:2], in0=gre[:, FR - 1:FR], in1=tabim[:, j, FR - 1:FR], op=ALU.mult),
                                           r=[gre, tabim], w=[cx])
                                    yield
                                    P.dve(lambda e: e.scalar_tensor_tensor(out=car[:, j, 0:1], in0=gre[:, FR - 1:FR], scalar=tabre[:, j, FR - 1:FR],
                                                                            in1=cx[:, 0:1], op0=ALU.mult, op1=ALU.add), r=[gre, tabre, cx, gim], w=[cj])
                                    P.dve(lambda e: e.scalar_tensor_tensor(out=car[:, j, 1:2], in0=gim[:, FR - 1:FR], scalar=tabre[:, j, FR - 1:FR],
                                                                            in1=cx[:, 1:2], op0=ALU.mult, op1=ALU.subtract), r=[gim, tabre, cx, gre], w=[cj])
                                    yield
                                    P.pe(lambda e: e.matmul(yacc[:, c0:c1], lhsT=CT[:, j, 0, :], rhs=b1[:, :], start=first, stop=False,
                                                            skip_group_check=True), r=[CT, b1], w=[yacc])
                                    P.pe(lambda e: e.matmul(yacc[:, c0:c1], lhsT=CT[:, j, 0, :], rhs=b2[:, :], start=False, stop=False,
                                                            skip_group_check=True), r=[CT, b2], w=[yacc])
                                    P.pe(lambda e: e.matmul(yacc[:, c0:c1], lhsT=CT[:, j, 1, :], rhs=b3[:, :], start=False, stop=False,
                                                            skip_group_check=True), r=[CT, b3], w=[yacc])
                                    P.pe(lambda e: e.matmul(yacc[:, c0:c1], lhsT=CTn[:, j, :], rhs=b4[:, :], start=False, stop=False,
                                                            skip_group_check=True), r=[CTn, b4], w=[yacc])
                                    yield

                                rr['An'] = 2
                                rr['A'] = 0
                                bps = [(PB[1], PB[2]), (PB[3], PB[4])]
                                rr['An'] = 1
                                rr['A'] = 0
                                for k in range(4):
                                    yacc = PB[5 + k % 2]
                                    for f in range(2):
                                        for pr in range(2):
                                            ga = s5_chain(4 * k + 2 * pr, k, f, WS[0], yacc, (f == 0 and pr == 0), bps[0])
                                            gb = s5_chain(4 * k + 2 * pr + 1, k, f, WS[1], yacc, False, bps[1])
                                            alive = [ga, gb]
                                            while alive:
                                                for g_ in list(alive):
                                                    if next(g_, 'done') == 'done':
                                                        alive.remove(g_)
                                    for f in range(2):
                                        P.pe(lambda e, k=k, f=f, yacc=yacc: e.matmul(yacc[:, f * FR:(f + 1) * FR], lhsT=DT[:, k, :],
                                                                                   rhs=uT[:, k, f * FR:(f + 1) * FR], start=False, stop=True,
                                                                                   skip_group_check=True), r=[DT, uT], w=[yacc])
                                    for f in range(2):
                                        gelu_to(yact[:, k, f * FR:(f + 1) * FR], yact, yacc[:, f * FR:(f + 1) * FR], yacc, scw, [128, FR],
                                                temps=(WS[0]['t'][0], WS[0]['t'][1]))
                                rr['An'] = 3
                            P.barrier()
                            wtl, wvl = wload('ssm_w_glu', l, 0, 4, 0, 1024)
                            wtg, wvg = wload('ssm_w_glu', l, 0, 4, 1024, 1024)
                            gv = gate_view(0)
                            ys = [P.tile("ys", [128, TT], F32, sc) for _ in range(1)]
                            for k in range(8):
                                pl = psA()
                                fm_mm(pl, wvl, (k * 128, k * 128 + 128), lambda kk: yact[:, kk, :], 4, rd=[wtl, yact])
                                pg_ = psA()
                                fm_mm(pg_, wvg, (k * 128, k * 128 + 128), lambda kk: yact[:, kk, :], 4, rd=[wtg, yact])
                                y_ = ys[0]
                                P.act(lambda e, pg_=pg_, y_=y_: e.activation(out=y_[:, :], in_=pg_[:, :], func=AF.Sigmoid), r=[pg_], w=[y_])
                                P.dve(lambda e, pl=pl, y_=y_: e.tensor_tensor(out=y_[:, :], in0=pl[:, :], in1=y_[:, :], op=ALU.mult),
                                      r=[pl, y_], w=[y_])
                                acc_mixed(k, y_[:, :], y_, sc, l, gv, 0)
                        P.barrier()
                    if F_C:
                        with ExitStack() as sc:
                            ccs = P.tile("ccs", [128, 4, TT], F32, sc)
                            zb = P.tile("zb", [128, 4, TT + 2], F32, sc)
                            cbz = P.tile("cbz", [128, 4, TT], BF16, sc)
                            P.pool(lambda e: e.tensor_copy(out=zb[:, :, 0:2], in_=halo[:, :, :]), r=[halo], w=[zb])
                            wt, wv = wload('w_in', l, 0, 8, C_CC, 512)
                            for k in range(4):
                                pt = psA()
                                fm_mm(pt, wv, (k * 128, k * 128 + 128), lambda kk: hT[:, kk, :], 8, rd=[wt, hT])
                                P.act(lambda e, k=k, pt=pt: e.activation(out=ccs[:, k, :], in_=pt[:, :], func=AF.Copy),
                                      r=[pt], w=[ccs])
                            wt, wv = wload('w_in', l, 0, 8, C_CX, 512)
                            for k in range(4):
                                pt = psA()
                                fm_mm(pt, wv, (k * 128, k * 128 + 128), lambda kk: hT[:, kk, :], 8, rd=[wt, hT])
                                P.dve(lambda e, k=k, pt=pt: e.tensor_tensor(out=zb[:, k, 2:TT + 2], in0=pt[:, :],
                                                                            in1=ccs[:, k, :], op=ALU.mult),
                                      r=[pt, ccs], w=[zb])
                            for k in range(4):
                                P.dve(lambda e, k=k: e.tensor_scalar(out=ccs[:, k, :], in0=zb[:, k, 2:TT + 2],
                                                                     scalar1=cw[:, 2, k:k + 1], scalar2=None, op0=ALU.mult),
                                      r=[zb, cw], w=[ccs])
                                P.dve(lambda e, k=k: e.scalar_tensor_tensor(out=ccs[:, k, :], in0=zb[:, k, 1:TT + 1],
                                                                            scalar=cw[:, 1, k:k + 1], in1=ccs[:, k, :],
                                                                            op0=ALU.mult, op1=ALU.add),
                                      r=[zb, cw, ccs], w=[ccs])
                                P.dve(lambda e, k=k: e.scalar_tensor_tensor(out=ccs[:, k, :], in0=zb[:, k, 0:TT],
                                                                            scalar=cw[:, 0, k:k + 1], in1=ccs[:, k, :],
                                                                            op0=ALU.mult, op1=ALU.add),
                                      r=[zb, cw, ccs], w=[ccs])
                            P.pool(lambda e: e.tensor_copy(out=halo[:, :, :], in_=zb[:, :, TT:TT + 2]), r=[zb], w=[halo])
                            wt, wv = wload('w_in', l, 0, 8, C_CB, 512)
                            for k in range(4):
                                pt = psA()
                                fm_mm(pt, wv, (k * 128, k * 128 + 128), lambda kk: hT[:, kk, :], 8, rd=[wt, hT])
                                P.dve(lambda e, k=k, pt=pt: e.tensor_tensor(out=cbz[:, k, :], in0=pt[:, :],
                                                                            in1=ccs[:, k, :], op=ALU.mult),
                                      r=[pt, ccs], w=[cbz])
                            wto, wvo = wload('conv_w_out', l, 0, 4, 0, 1024)
                            gv = gate_view(1)
                            for k in range(8):
                                pt = psA()
                                fm_mm(pt, wvo, (k * 128, k * 128 + 128), lambda kk: cbz[:, kk, :], 4, rd=[wto, cbz])
                                acc_mixed(k, pt[:, :], pt, sc, l, gv, 1)
                        P.barrier()
                    if F_N:
                        with ExitStack() as sc:
                            qTn = P.tile("qTn", [128, 4, TT], BF16, sc)
                            gsb = P.tile("gsb", [128, 4, 24], F32, sc)
                            scp = ExitStack()
                            nq = [P.tile("nq", [128, TT], F32, scp) for _ in range(2)]
                            ns = [P.tile("nsq", [128, TT], F32, scp) for _ in range(2)]
                            nrr = [0]

                            def headnorm(src_tl, ncols, gcol, out_ap, out_tl, rows=slice(0, 128)):
                                nrr[0] ^= 1
                                qs, sq = nq[nrr[0]], ns[nrr[0]]
                                P.act(lambda e: e.activation(out=qs[:, 0:ncols], in_=src_tl[:, 0:ncols], func=AF.Copy), r=[src_tl], w=[qs])
                                P.act(lambda e: e.activation(out=sq[:, 0:ncols], in_=src_tl[:, 0:ncols], func=AF.Square), r=[src_tl], w=[sq])
                                pss = psA()
                                P.pe(lambda e: e.matmul(pss[:, 0:ncols], lhsT=bd1[:, :], rhs=sq[:, 0:ncols], start=True, stop=True),
                                     r=[bd1, sq], w=[pss])
                                P.act(lambda e: e.activation(out=sq[:, 0:ncols], in_=pss[:, 0:ncols], func=AF.Sqrt, scale=1.0 / 64,
                                                             bias=epsb[:, 0:1]), r=[pss, epsb], w=[sq])
                                P.dve(lambda e: e.reciprocal(out=sq[:, 0:ncols], in_=sq[:, 0:ncols]), r=[sq], w=[sq])
                                P.dve(lambda e: e.scalar_tensor_tensor(out=out_ap, in0=qs[rows, 0:ncols], scalar=gcol, in1=sq[rows, 0:ncols],
                                                                       op0=ALU.mult, op1=ALU.mult), r=[qs, sq, gq, gk], w=[out_tl])

                            wt, wv = wload('w_in', l, 0, 8, C_Q, 512)
                            for g in range(4):
                                pt = psA()
                                for two in range(2):
                                    for k in range(8):
                                        c0 = two * 256 + g * 64
                                        P.pe(lambda e, k=k, c0=c0, two=two, pt=pt, wv=wv: e.matmul(
                                            pt[two * 64:(two + 1) * 64, :], lhsT=wv[:, k, c0:c0 + 64],
                                            rhs=hT[:, k, :], start=(k == 0), stop=(k == 7)), r=[wt, hT], w=[pt])
                                headnorm(pt, TT, gq[:, 0:1], qTn[:, g, :], qTn)
                            wt, wv = wload('w_in', l, 0, 8, C_KC, 512)
                            kcin = P.tile("kcin", [128, 16, 16], BF16, scp)
                            hid = P.tile("hid", [128, 2, 16], BF16, scp)
                            for w_ in range(2):
                                wt1, wv1 = wload('cmp_w1', l, w_ * 2048, 16, 0, 256)
                                for hh in range(2):
                                    pt = psA()
                                    cbase = w_ * 128 + hh * 64
                                    for two in range(2):
                                        for k in range(8):
                                            P.pe(lambda e, k=k, two=two, pt=pt, wv=wv, cbase=cbase: e.matmul(
                                                pt[two * 64:(two + 1) * 64, :], lhsT=wv[:, k, cbase:cbase + 64],
                                                rhs=hT[:, k, :], start=(k == 0), stop=(k == 7)), r=[wt, hT], w=[pt])
                                    for ph in range(2):
                                        rws = slice(ph * 64, ph * 64 + 64)
                                        P.dve(lambda e, ph=ph, rws=rws, pt=pt, w_=w_: e.tensor_tensor(
                                            out=kcin[rws, :, :],
                                            in0=pt[rws, :].rearrange("p (c l two) -> p c l two", c=16, l=16, two=2)[:, :, :, ph],
                                            in1=peT[rws, w_, :].rearrange("p (l two) -> p l two", two=2)[:, :, ph].unsqueeze(1).to_broadcast([64, 16, 16]),
                                            op=ALU.add), r=[pt, peT], w=[kcin])
                                    for jh in range(2):
                                        ph_ = psA()
                                        for l2 in range(16):
                                            P.pe(lambda e, l2=l2, jh=jh, ph_=ph_, wv1=wv1: e.matmul(
                                                ph_[:, 0:16], lhsT=wv1[:, l2, jh * 128:(jh + 1) * 128], rhs=kcin[:, :, l2],
                                                start=(l2 == 0), stop=(l2 == 15)), r=[wt1, kcin], w=[ph_])
                                        gelu_to(hid[:, jh, :], hid, ph_[:, 0:16], ph_, scp, [128, 16])
                                    pk = psA()
                                    for two in range(2):
                                        for jh in range(2):
                                            P.pe(lambda e, jh=jh, two=two, pk=pk, w_=w_: e.matmul(
                                                pk[two * 64:(two + 1) * 64, 0:16], lhsT=w2sb[:, w_, jh, :],
                                                rhs=hid[:, jh, :], start=(jh == 0), stop=(jh == 1)), r=[w2sb, hid], w=[pk])
                                    rws = slice(hh * 64, hh * 64 + 64)
                                    if w_ == 0:
                                        headnorm(pk, 16, gk[rws, 0:1], kcT[rws, i * 16:(i + 1) * 16], kcT, rows=rws)
                                    else:
                                        P.act(lambda e, pk=pk, rws=rws, i=i: e.activation(out=vcT[rws, i * 16:(i + 1) * 16], in_=pk[rws, 0:16],
                                                                                 func=AF.Copy), r=[pk], w=[vcT])
                            P.pe(lambda e: e.transpose(out=psT[:, 0:128], in_=vcT[:, :], identity=ident_b[:, :]), r=[vcT, ident_b], w=[psT])
                            P.dve(lambda e: e.tensor_copy(out=vcaug[:, :, 0:64], in_=psT[:, 0:128].rearrange("p (h d) -> p h d", h=2)),
                                  r=[psT], w=[vcaug])
                            pt = psA()
                            fm_mm(pt, wv, (256, 384), lambda kk: hT[:, kk, :], 8, rd=[wt, hT])
                            headnorm(pt, TT, gk[:, 1:2], ksT[:, t0:t0 + TT], ksT)
                            for s in range(4):
                                pt = psA()
                                for k in range(8):
                                    P.pe(lambda e, k=k, s=s, pt=pt, wv=wv: e.matmul(pt[:, 0:128], lhsT=hT[:, k, s * 128:(s + 1) * 128],
                                                                                   rhs=wv[:, k, 384:512], start=(k == 0), stop=(k == 7)),
                                         r=[wt, hT], w=[pt])
                                P.act(lambda e, s=s, pt=pt, i=i: e.activation(out=vs[:, i * 4 + s, :, 0:64],
                                                                         in_=pt[:, 0:128].rearrange("p (h d) -> p h d", h=2), func=AF.Copy),
                                      r=[pt], w=[vs])
                            wt, wv = wload('w_in', l, 0, 8, C_KW, 280)
                            pt = psA()
                            fm_mm(pt, wv, (0, 128), lambda kk: hT[:, kk, :], 8, rd=[wt, hT])
                            headnorm(pt, TT, gk[:, 2:3], kwT[:, t0:t0 + TT], kwT)
                            for s in range(4):
                                pt = psA()
                                for k in range(8):
                                    P.pe(lambda e, k=k, s=s, pt=pt, wv=wv: e.matmul(pt[:, 0:152], lhsT=hT[:, k, s * 128:(s + 1) * 128],
                                                                                   rhs=wv[:, k, 128:280], start=(k == 0), stop=(k == 7)),
                                         r=[wt, hT], w=[pt])
                                P.act(lambda e, s=s, pt=pt, i=i: e.activation(out=vw[:, i * 4 + s, :, 0:64],
                                                                         in_=pt[:, 0:128].rearrange("p (h d) -> p h d", h=2), func=AF.Copy),
                                      r=[pt], w=[vw])
                                P.act(lambda e, s=s, pt=pt: e.activation(out=gsb[:, s, :], in_=pt[:, 128:152], func=AF.Sigmoid),
                                      r=[pt], w=[gsb])
                            scp.close()
                            P.barrier()
                            oT = P.tile("oT", [128, 4, TT], BF16, sc)
                            o_accs = [P.tile("o_acc", [128, 8, 64], F32, sc) for _ in range(2)]
                            o_bf = P.tile("o_bf", [128, 512], BF16, sc)
                            eb = [P.tile("eb", [128, 512], BF16, sc) for _ in range(3)]
                            pb = [P.tile("pb", [128, 512], BF16, sc) for _ in range(7)]
                            mtris = [P.tile("mtri", [128, 128], BF16, sc) for _ in range(2)]
                            mTa = [P.tile("mTa", [64, 128], BF16, sc) for _ in range(2)]
                            den = P.tile("den", [128, 4], F32, sc)
                            scl = P.tile("scl", [128, 4], F32, sc)
                            den_c = P.tile("den_c", [128, 4], F32, sc)
                            scl_c = P.tile("scl_c", [128, 4], F32, sc)
                            imp = P.tile("imp", [128, NBLK], F32, sc)
                            imp3 = P.tile("imp3", [128, NBLK], F32, sc)
                            m8 = P.tile("m8", [128, 8], F32, sc)
                            mskb = P.tile("mskb", [128, NBLK], BF16, sc)
                            rre = {'e': 0, 'p': 0, 'm': 0}

                            def nxt_e():
                                rre['e'] = (rre['e'] + 1) % len(eb)
                                return eb[rre['e']]

                            def nxt_p():
                                rre['p'] = (rre['p'] + 1) % len(pb)
                                return pb[rre['p']]

                            def v4(t):
                                return t[:, :].rearrange("p (g q) -> p g q", g=4)

                            def finalize(Ot, h, s, branch):
                                oa = o_accs[s % 2]
                                Ov = Ot[:, 0:260].rearrange("p (g w) -> p g w", g=4)
                                P.dve(lambda e: e.tensor_scalar(out=den[:, :], in0=Ov[:, :, 64], scalar1=1e-30, scalar2=None, op0=ALU.max),
                                      r=[Ot], w=[den])
                                P.dve(lambda e: e.reciprocal(out=den[:, :], in_=den[:, :]), r=[den], w=[den])
                                gsel = gsb[:, s, :].rearrange("p (hd b) -> p hd b", b=3)[:, h * 4:(h + 1) * 4, branch]
                                P.dve(lambda e: e.tensor_tensor(out=scl[:, :], in0=den[:, :], in1=gsel, op=ALU.mult), r=[den, gsb], w=[scl])
                                for g in range(4):
                                    hd = h * 4 + g
                                    P.dve(lambda e, g=g, hd=hd: e.scalar_tensor_tensor(out=oa[:, hd, :], in0=Ov[:, g, 0:64], scalar=scl[:, g:g + 1],
                                                                                       in1=oa[:, hd, :], op0=ALU.mult, op1=ALU.add),
                                          r=[Ot, scl, oa], w=[oa])

                            def cmp_chain(s, h):
                                qi = i * 4 + s
                                tq = qi * 128
                                oa = o_accs[s % 2]
                                rows = slice(h * 64, h * 64 + 64)
                                Q = qTn[rows, :, s * 128:(s + 1) * 128]
                                sT = psS()
                                P.pe(lambda e: e.matmul(v4(sT), lhsT=kcT[rows, :], rhs=Q, start=True, stop=True), r=[kcT, qTn], w=[sT])
                                e_ = nxt_e()
                                P.act(lambda e: e.activation(out=e_[:, :], in_=sT[:, :], func=AF.Exp), r=[sT], w=[e_])
                                p_ = nxt_p()
                                P.pool(lambda e: e.affine_select(out=v4(p_), in_=v4(e_), pattern=[[0, 4], [1, 128]], compare_op=ALU.is_ge,
                                                                 fill=0.0, base=tq - 31, channel_multiplier=-32), r=[e_], w=[p_])
                                yield
                                Ot = PB[2]
                                gsel = gsb[:, s, :].rearrange("p (hd b) -> p hd b", b=3)[:, h * 4:(h + 1) * 4, 0]
                                for hf in range(2):
                                    for g2 in range(2):
                                        g = hf * 2 + g2
                                        P.pe(lambda e, g=g, g2=g2: e.matmul(Ot[:, g2 * WV:(g2 + 1) * WV], lhsT=p_[:, g * 128:(g + 1) * 128],
                                                                           rhs=vcaug[:, h, :], start=(g2 == 0), stop=True, skip_group_check=True),
                                             r=[p_, vcaug], w=[Ot])
                                    yield
                                    P.dve(lambda e, hf=hf: e.tensor_scalar(
                                        out=den_c[:, 2 * hf:2 * hf + 2], in0=Ot[:, 0:2 * WV].rearrange("p (g w) -> p g w", g=2)[:, :, 64],
                                        scalar1=1e-30, scalar2=None, op0=ALU.max), r=[Ot], w=[den_c])
                                    P.dve(lambda e, hf=hf: e.reciprocal(out=den_c[:, 2 * hf:2 * hf + 2], in_=den_c[:, 2 * hf:2 * hf + 2]),
                                          r=[den_c], w=[den_c])
                                    P.dve(lambda e, hf=hf: e.tensor_tensor(out=scl_c[:, 2 * hf:2 * hf + 2], in0=den_c[:, 2 * hf:2 * hf + 2],
                                                                           in1=gsel[:, 2 * hf:2 * hf + 2], op=ALU.mult), r=[den_c, gsb], w=[scl_c])
                                    yield
                                    for g2 in range(2):
                                        g = hf * 2 + g2
                                        o0 = g2 * WV
                                        hd = h * 4 + g
                                        P.dve(lambda e, g=g, hd=hd, o0=o0: e.tensor_scalar(
                                            out=oa[:, hd, :], in0=Ot[:, o0:o0 + 64], scalar1=scl_c[:, g:g + 1], scalar2=None, op0=ALU.mult),
                                            r=[Ot, scl_c], w=[oa])
                                        if g == 0:
                                            P.dve(lambda e, o0=o0: e.tensor_scalar(
                                                out=imp[:, :], in0=Ot[:, o0 + 65:o0 + WV], scalar1=den_c[:, 0:1], scalar2=None, op0=ALU.mult),
                                                r=[Ot, den_c], w=[imp])
                                        else:
                                            P.dve(lambda e, g=g, o0=o0: e.scalar_tensor_tensor(
                                                out=imp[:, :], in0=Ot[:, o0 + 65:o0 + WV], scalar=den_c[:, g:g + 1], in1=imp[:, :],
                                                op0=ALU.mult, op1=ALU.add), r=[Ot, den_c, imp], w=[imp])
                                    yield
                                a0 = NBLK - 2 * qi
                                P.dve(lambda e: e.tensor_tensor(out=imp[:, :], in0=imp[:, :], in1=TA[:, a0:a0 + NBLK], op=ALU.mult),
                                      r=[imp, TA], w=[imp])
                                P.dve(lambda e: e.tensor_tensor(out=imp[:, :], in0=imp[:, :], in1=TB[:, a0:a0 + NBLK], op=ALU.add),
                                      r=[imp, TB], w=[imp])
                                P.dve(lambda e: e.memset(imp[:, 0:1], 1e4), r=[], w=[imp])
                                yield
                                P.dve(lambda e: e.max(out=m8[:, :], in_=imp[:, :]), r=[imp], w=[m8])
                                P.dve(lambda e: e.match_replace(out=imp3[:, :], in_to_replace=m8[:, :], in_values=imp[:, :], imm_value=-3e38),
                                      r=[imp, m8], w=[imp3])
                                yield
                                P.dve(lambda e: e.max(out=m8[:, :], in_=imp3[:, :]), r=[imp3], w=[m8])
                                P.dve(lambda e: e.tensor_scalar(out=mskb[:, :], in0=imp[:, :], scalar1=m8[:, 7:8], scalar2=None, op0=ALU.is_ge),
                                      r=[imp, m8], w=[mskb])
                                yield
                                P.pe(lambda e: e.transpose(out=psT[0:NBLK, 0:128], in_=mskb[:, :], identity=ident_b[:, :]),
                                     r=[mskb, ident_b], w=[psT])
                                P.dve(lambda e: e.tensor_copy(out=mTa[h][0:NBLK, :], in_=psT[0:NBLK, 0:128]), r=[psT], w=[mTa[h]])
                                yield

                            def stage_a(kind, kt, s, h):
                                qi = i * 4 + s
                                rows = slice(h * 64, h * 64 + 64)
                                Q = qTn[rows, :, s * 128:(s + 1) * 128]
                                sT = psS()
                                if kind == 'sel':
                                    P.pe(lambda e: e.matmul(v4(sT), lhsT=ksT[rows, kt * 128:(kt + 1) * 128], rhs=Q, start=True, stop=True),
                                         r=[ksT, qTn], w=[sT])
                                    mx = psA()
                                    P.pe(lambda e: e.matmul(mx[:, 0:128], lhsT=Em[:, kt, :], rhs=mTa[h][0:NBLK, :], start=True, stop=True),
                                         r=[Em, mTa[h]], w=[mx])
                                    e_ = nxt_e()
                                    P.act(lambda e: e.activation(out=e_[:, :], in_=sT[:, :], func=AF.Exp), r=[sT], w=[e_])
                                    p_ = nxt_p()
                                    if kt == qi:
                                        rre['m'] ^= 1
                                        mtri = mtris[rre['m']]
                                        P.dve(lambda e: e.tensor_tensor(out=mtri[:, :], in0=mx[:, 0:128], in1=tri_le[:, :], op=ALU.mult),
                                              r=[mx, tri_le], w=[mtri])
                                        P.pool(lambda e: e.tensor_tensor(out=v4(p_), in0=v4(e_), in1=mtri[:, :].unsqueeze(1).to_broadcast([128, 4, 128]),
                                                                         op=ALU.mult), r=[e_, mtri], w=[p_])
                                    else:
                                        P.dve(lambda e: e.tensor_tensor(out=v4(p_), in0=v4(e_), in1=mx[:, 0:128].unsqueeze(1).to_broadcast([128, 4, 128]),
                                                                        op=ALU.mult), r=[e_, mx], w=[p_])
                                    return p_
                                P.pe(lambda e: e.matmul(v4(sT), lhsT=kwT[rows, kt * 128:(kt + 1) * 128], rhs=Q, start=True, stop=True),
                                     r=[kwT, qTn], w=[sT])
                                p_ = nxt_p()
                                if kt == qi or kt == qi - 4:
                                    msk_ = tri_le if kt == qi else tri_gt
                                    e_ = nxt_e()
                                    P.act(lambda e: e.activation(out=e_[:, :], in_=sT[:, :], func=AF.Exp), r=[sT], w=[e_])
                                    P.pool(lambda e: e.tensor_tensor(out=v4(p_), in0=v4(e_), in1=msk_[:, :].unsqueeze(1).to_broadcast([128, 4, 128]),
                                                                     op=ALU.mult), r=[e_, msk_], w=[p_])
                                else:
                                    P.act(lambda e: e.activation(out=p_[:, :], in_=sT[:, :], func=AF.Exp), r=[sT], w=[p_])
                                return p_

                            def stage_b(kind, kt, s, h, p_, first, last):
                                Ot = PB[5] if kind == 'sel' else PB[6]
                                vv = vs if kind == 'sel' else vw
                                for g in range(4):
                                    P.pe(lambda e, g=g: e.matmul(Ot[:, g * 65:(g + 1) * 65], lhsT=p_[:, g * 128:(g + 1) * 128], rhs=vv[:, kt, h, :],
                                                                 start=(first and g == 0), stop=last, skip_group_check=True), r=[p_, vv], w=[Ot])
                                if last:
                                    finalize(Ot, h, s, 1 if kind == 'sel' else 2)

                            LOOK = 4
                            rr['An'] = 2
                            rr['A'] = 0
                            pairs = [(s, h) for s in range(4) for h in range(2)]
                            chains = [cmp_chain(s, h) for (s, h) in pairs]
                            for _ in chains[0]:
                                pass
                            for pi, (s, h) in enumerate(pairs):
                                qi = i * 4 + s
                                nxc = chains[pi + 1] if pi + 1 < len(pairs) else None
                                kts = list(range(max(0, qi - 4), qi + 1))
                                items = [('sel', kt, kt == 0, kt == qi) for kt in range(qi + 1)]
                                items += [('win', kt, kt == kts[0], kt == qi) for kt in kts]
                                pend = []
                                for (kind, kt, first, last) in items:
                                    p_ = stage_a(kind, kt, s, h)
                                    pend.append((kind, kt, p_, first, last))
                                    if nxc is not None:
                                        next(nxc, None)
                                    if len(pend) > LOOK:
                                        k_, kt_, pp_, f_, l_ = pend.pop(0)
                                        stage_b(k_, kt_, s, h, pp_, f_, l_)
                                while pend:
                                    k_, kt_, pp_, f_, l_ = pend.pop(0)
                                    stage_b(k_, kt_, s, h, pp_, f_, l_)
                                if nxc is not None:
                                    for _ in nxc:
                                        pass
                                if h == 1:
                                    oa = o_accs[s % 2]
                                    P.act(lambda e, oa=oa: e.activation(out=o_bf[:, :], in_=oa[:, :, :].rearrange("p h d -> p (h d)"), func=AF.Copy),
                                          r=[oa], w=[o_bf])
                                    for c in range(4):
                                        P.pe(lambda e, c=c: e.transpose(out=psT[:, c * 128:(c + 1) * 128], in_=o_bf[:, c * 128:(c + 1) * 128],
                                                                        identity=ident_b[:, :]), r=[o_bf, ident_b], w=[psT])
                                    P.dve(lambda e, s=s: e.tensor_copy(out=oT[:, :, s * 128:(s + 1) * 128],
                                                                       in_=psT[:, 0:512].rearrange("p (c q) -> p c q", c=4)), r=[psT], w=[oT])
                            rr['An'] = 3
                            wto, wvo = wload('nsa_w_o', l, 0, 4, 0, 1024)
                            gv = gate_view(2)
                            for k in range(8):
                                pt = psA()
                                fm_mm(pt, wvo, (k * 128, k * 128 + 128), lambda kk: oT[:, kk, :], 4, rd=[wto, oT])
                                acc_mixed(k, pt[:, :], pt, sc, l, gv, 2)
                        P.barrier()
                    with ExitStack() as sc:
                        mb = P.tile("mixed_bf", [128, 8, TT], BF16, sc)
                        for k in range(8):
                            P.act(lambda e, k=k: e.activation(out=mb[:, k, :], in_=mixed[:, k, :], func=AF.Copy),
                                  r=[mixed], w=[mb])
                        for half in range(2):
                            wt, wv = wload('w_out', l, 0, 8, half * 512, 512)
                            for s in range(4):
                                pt = psA()
                                for k in range(8):
                                    P.pe(lambda e, k=k, s=s, pt=pt, wv=wv: e.matmul(
                                        pt[:, :], lhsT=mb[:, k, s * 128:(s + 1) * 128], rhs=wv[:, k, :],
                                        start=(k == 0), stop=(k == 7)), r=[mb, wt], w=[pt])
                                P.dve(lambda e, s=s, pt=pt, half=half: e.tensor_tensor(
                                    out=xt[:, s, half * 512:(half + 1) * 512], in0=pt[:, :],
                                    in1=xt[:, s, half * 512:(half + 1) * 512], op=ALU.add), r=[pt, xts[s]], w=[xts[s]])
                    P.barrier()
                if F_F:
                    with ExitStack() as sc:
                        rmsnorm_T(1, sc)
                        aT = P.tile("aT", [128, 22, TT], BF16, sc)
                        silu_t = [P.tile("silu", [128, TT], F32, sc) for _ in range(2)]
                        for gb in range(6):
                            nt_ = 4 if gb < 5 else 2
                            wtg, wvg = wload('ffn_w_gate_up', l, 0, 8, gb * 512, nt_ * 128)
                            wtu, wvu = wload('ffn_w_gate_up', l, 0, 8, DFF + gb * 512, nt_ * 128)
                            for jj in range(nt_):
                                j = gb * 4 + jj
                                pg = psA()
                                fm_mm(pg, wvg, (jj * 128, jj * 128 + 128), lambda kk: hT[:, kk, :], 8, rd=[wtg, hT])
                                pu = psA()
                                fm_mm(pu, wvu, (jj * 128, jj * 128 + 128), lambda kk: hT[:, kk, :], 8, rd=[wtu, hT])
                                sg = silu_t[j % 2]
                                P.act(lambda e, pg=pg, sg=sg: e.activation(out=sg[:, :], in_=pg[:, :], func=AF.Silu),
                                      r=[pg], w=[sg])
                                P.dve(lambda e, pu=pu, sg=sg, j=j: e.tensor_tensor(out=aT[:, j, :], in0=pu[:, :], in1=sg[:, :],
                                                                                  op=ALU.mult), r=[pu, sg], w=[aT])
                        for half in range(2):
                            accs = [PB[3], PB[4], PB[5], PB[6]]
                            for cg in range(3):
                                nk = 8 if cg < 2 else 6
                                wt, wv = wload('ffn_w_down', l, cg * 1024, nk, half * 512, 512)
                                for s in range(4):
                                    for kk in range(nk):
                                        j = cg * 8 + kk
                                        P.pe(lambda e, s=s, kk=kk, j=j, wv=wv: e.matmul(
                                            accs[s][:, :], lhsT=aT[:, j, s * 128:(s + 1) * 128], rhs=wv[:, kk, :],
                                            start=(j == 0), stop=(j == 21)), r=[aT, wt], w=[accs[s]])
                            for s in range(4):
                                P.dve(lambda e, s=s, half=half: e.tensor_tensor(
                                    out=xt[:, s, half * 512:(half + 1) * 512], in0=accs[s][:, :],
                                    in1=xt[:, s, half * 512:(half + 1) * 512], op=ALU.add), r=[accs[s], xts[s]], w=[xts[s]])
                                if half == 1:
                                    P.dma('act', out_d[t0 + s * 128:t0 + (s + 1) * 128, :], xt[:, s, :], r=[xts[s]], w=[b_xd[i][s]])
                    P.barrier()
                if not F_F:
                    for s_ in range(4):
                        P.dma('act', out_d[t0 + s_ * 128:t0 + (s_ + 1) * 128, :], xt[:, s_, :], r=[xts[s_]], w=[b_xd[i][s_]])
            for i in range(NT):
                step(l, i)
        P.finish('sp')
        nc._reg = P.reg
        print('sbuf remaining', nc.sbuf_bytes_remaining)
        P.emit()
    return nc


def conv_branch(P, sc, l, i, env):
    raise NotImplementedError


_NC_CACHE = {}


def kernel(**inputs):
    x = np.ascontiguousarray(np.asarray(inputs['x'], dtype=np.float32))
    B, S, _ = x.shape
    key = (S, 4)
    if key not in _NC_CACHE:
        _NC_CACHE[key] = build(S, 4, "fcsn")
    nc = _NC_CACHE[key]
    in_maps = []
    for c in range(8):
        m = {'x': x[c % B]}
        for k in PARAM_SHAPES:
            m[k] = np.ascontiguousarray(np.asarray(inputs[k], dtype=np.float32))
        in_maps.append(m)
    res = run_bass_kernel_spmd(nc, in_maps, core_ids=list(range(8)))
    out = np.stack([res.results[b]['out'] for b in range(B)], axis=0)
    return out.astype(np.float32)
```

```python
import numpy as np
from contextlib import ExitStack
import concourse.bass as bass
import concourse.mybir as mybir
from concourse.bass_utils import run_bass_kernel_spmd

ALU = mybir.AluOpType
AF = mybir.ActivationFunctionType
F32 = mybir.dt.float32
BF16 = mybir.dt.bfloat16

ENGS = ('pe', 'dve', 'act', 'pool', 'sp')
DQ = ('sp', 'pool', 'act')
NDS = 8


class Buf:
    __slots__ = ('w', 'r')

    def __init__(self):
        self.w = None
        self.r = {}


class Tl:
    def __init__(self, t, b=None):
        self.t = t
        self.b = b if b is not None else Buf()

    def __getitem__(self, k):
        return self.t[k]


def _bufs(xs):
    return [x.b if isinstance(x, Tl) else x for x in xs]


class Prog:
    def __init__(self, nc, es):
        self.nc = nc
        self.es = es
        self.streams = {e: [] for e in ENGS}
        self.cnt = {e: 0 for e in ENGS}
        self.sem = {e: es.enter_context(nc.semaphore('s_' + e)) for e in ENGS}
        self.seen = {e: {} for e in ENGS}
        self.dsem = {e: [es.enter_context(nc.semaphore('d_%s%d' % (e, i))) for i in range(NDS)] for e in DQ}
        self.dval = {e: [0] * NDS for e in DQ}
        self.drr = {e: 0 for e in DQ}
        self.uid = 0
        self.noalias = False
        self.stepcnt = {}
        self.tcache = {}
        self.reg = {}

    def tile(self, name, shape, dtype, es=None):
        self.uid += 1
        if es is not None and self.noalias:
            c = self.stepcnt.get(name, 0)
            self.stepcnt[name] = c + 1
            key = (name, c, tuple(shape), str(dtype))
            if key not in self.tcache:
                self.reg.setdefault(name, []).append('%s_%d' % (name, self.uid))
                self.tcache[key] = Tl(self.es.enter_context(
                    self.nc.sbuf_tensor('%s_%d' % (name, self.uid), list(shape), dtype)))
            return self.tcache[key]
        es = es or self.es
        self.reg.setdefault(name, []).append('%s_%d' % (name, self.uid))
        return Tl(es.enter_context(self.nc.sbuf_tensor('%s_%d' % (name, self.uid), list(shape), dtype)))

    def new_step(self):
        self.stepcnt = {}

    def ptile(self, name, shape, dtype=F32):
        return Tl(self.es.enter_context(self.nc.psum_tensor(name, list(shape), dtype)))

    def _collect(self, eng, reads, writes):
        deps = {}

        def add(h):
            if h is None:
                return
            key, sem, val = h
            if key == eng and eng == 'pe':
                return
            cur = deps.get(key)
            if cur is None or cur[1] < val:
                deps[key] = (sem, val)
        for b in reads:
            add(b.w)
        for b in writes:
            add(b.w)
            for h in b.r.values():
                add(h)
        waits = []
        seen = self.seen[eng]
        for key, (sem, val) in deps.items():
            if seen.get(key, 0) < val:
                waits.append((sem, val))
                seen[key] = val
        return waits

    def _update(self, h, reads, writes):
        for b in reads:
            cur = b.r.get(h[0])
            if cur is None or cur[2] < h[2]:
                b.r[h[0]] = h
        for b in writes:
            b.w = h
            b.r = {}

    def op(self, eng, fn, r=(), w=()):
        reads = _bufs(r)
        writes = _bufs(w)
        waits = self._collect(eng, reads, writes)
        self.cnt[eng] += 1
        h = (eng, self.sem[eng], self.cnt[eng])
        self.streams[eng].append((waits, fn, (self.sem[eng], 1)))
        self._update(h, reads, writes)
        return h

    def pe(self, fn, r=(), w=()):
        return self.op('pe', fn, r, w)

    def dve(self, fn, r=(), w=()):
        return self.op('dve', fn, r, w)

    def act(self, fn, r=(), w=()):
        return self.op('act', fn, r, w)

    def pool(self, fn, r=(), w=()):
        return self.op('pool', fn, r, w)

    def dma(self, eng, out, in_, r=(), w=(), **kw):
        reads = _bufs(r)
        writes = _bufs(w)
        waits = self._collect(eng, reads, writes)
        i = self.drr[eng]
        self.drr[eng] = (i + 1) % NDS
        sem = self.dsem[eng][i]
        prev = self.dval[eng][i]
        key = ('d', eng, i)
        if prev > 0 and self.seen[eng].get(key, 0) < prev:
            waits.append((sem, prev))
            self.seen[eng][key] = prev
        val = prev + 16
        self.dval[eng][i] = val
        h = (key, sem, val)
        self.streams[eng].append((waits, lambda e: e.dma_start(out=out, in_=in_, **kw), (sem, 16)))
        self._update(h, reads, writes)
        return h

    def barrier(self):
        ce = ('pe', 'dve', 'act', 'pool')
        for e in ce:
            waits = []
            for f in ce:
                if f == e or self.cnt[f] == 0:
                    continue
                if self.seen[e].get(f, 0) < self.cnt[f]:
                    waits.append((self.sem[f], self.cnt[f]))
                    self.seen[e][f] = self.cnt[f]
            if waits:
                self.streams[e].append((waits, None, None))

    def finish(self, eng='sp'):
        waits = []
        for q in DQ:
            for i in range(NDS):
                if self.dval[q][i] > 0:
                    waits.append((self.dsem[q][i], self.dval[q][i]))
        self.streams[eng].append((waits, None, None))

    def emit(self, seg=800):
        def replay(name, e, lo, hi):
            for waits, fn, inc in self.streams[name][lo:hi]:
                for sem, val in waits:
                    e.wait_ge(sem, val)
                if fn is not None:
                    fn(e).then_inc(inc[0], inc[1])

        with self.nc.Block() as block:
            starters = {'pe': block.tensor, 'dve': block.vector, 'act': block.scalar,
                        'pool': block.gpsimd, 'sp': block.sync}
            for name in ENGS:
                n = len(self.streams[name])
                for lo in range(0, max(n, 1), seg):
                    starters[name](lambda e, name=name, lo=lo: replay(name, e, lo, min(lo + seg, n)))


D = 1024
DFF = 2816
INW = 6424
TT = 512
C_U, C_CB, C_CC, C_CX, C_Q = 0, 512, 1024, 1536, 2048
C_KC, C_VC, C_KS, C_VS, C_KW, C_VW, C_NG, C_MG = 2560, 2688, 2816, 2944, 3072, 3200, 3328, 3352
GELU_K = 1.5957691216057308

PARAM_SHAPES = {
    'mix_norm_g': (D,), 'w_in': (D, INW), 'ssm_lam_re': (32, 64), 'ssm_lam_im': (32, 64),
    'ssm_b_re': (32, 64, 16), 'ssm_b_im': (32, 64, 16), 'ssm_c_re': (32, 16, 64), 'ssm_c_im': (32, 16, 64),
    'ssm_d': (512,), 'ssm_log_dt': (32,), 'ssm_w_glu': (512, 2048), 'conv_w': (3, 512),
    'conv_w_out': (512, D), 'q_norm_g': (64,), 'k_norm_g': (3, 64), 'cmp_pe': (2, 32, 64),
    'cmp_w1': (2, 2048, 256), 'cmp_w2': (2, 256, 64), 'nsa_w_o': (512, D), 'w_out': (D, D),
    'ffn_norm_g': (D,), 'ffn_w_gate_up': (D, 2 * DFF), 'ffn_w_down': (DFF, D),
}
BIGW = {'w_in': (D, INW), 'ssm_w_glu': (512, 2048), 'conv_w_out': (512, D), 'cmp_w1': (4096, 256),
        'cmp_w2': (512, 64), 'nsa_w_o': (512, D), 'w_out': (D, D), 'ffn_w_gate_up': (D, 2 * DFF),
        'ffn_w_down': (DFF, D)}


def build(S=4096, depth=4, flags="fcsn", noalias=False):
    nc = bass.Bass("TRN2", target_bir_lowering=False)
    NT = S // TT
    NKT = S // 128
    NBLK = S // 64
    NCMP = S // 32
    F_S, F_C, F_N, F_F = ('s' in flags), ('c' in flags), ('n' in flags), ('f' in flags)
    MIX = F_S or F_C or F_N
    x_d = nc.dram_tensor("x", [S, D], F32, kind="ExternalInput")
    out_d = nc.dram_tensor("out", [S, D], F32, kind="ExternalOutput")
    prm = {k: nc.dram_tensor(k, [depth] + list(v), F32, kind="ExternalInput") for k, v in PARAM_SHAPES.items()}
    wbd = {k: nc.dram_tensor(k + "_b", [depth] + list(v), BF16, kind="Internal") for k, v in BIGW.items()}

    with ExitStack() as es:
        P = Prog(nc, es)
        P.noalias = noalias
        nc_es = es

        def AP2(ap):
            return ap

        b_wconv = {}
        used_w = set()
        if MIX:
            used_w |= {'w_in', 'w_out'}
        if F_S:
            used_w |= {'ssm_w_glu'}
        if F_C:
            used_w |= {'conv_w_out'}
        if F_N:
            used_w |= {'cmp_w1', 'cmp_w2', 'nsa_w_o'}
        if F_F:
            used_w |= {'ffn_w_gate_up', 'ffn_w_down'}
        for l in range(depth):
            for k, (rows, cols) in BIGW.items():
                if k not in used_w:
                    continue
                b = Buf()
                b_wconv[(k, l)] = b
                src = prm[k][l]
                if k in ('cmp_w1', 'cmp_w2'):
                    src = src.rearrange("a r c -> (a r) c")
                dst = wbd[k][l]
                rstep = max(16, min(rows, ((1 << 20) // cols) // 16 * 16))
                hs = []
                for r0 in range(0, rows, rstep):
                    r1 = min(rows, r0 + rstep)
                    hs.append(P.dma('pool', dst[r0:r1, :], src[r0:r1, :]))
                b.w = None
                b.r = {}
                b_wconv[(k, l)] = [Buf() for _ in hs]
                for bb, h in zip(b_wconv[(k, l)], hs):
                    bb.w = h

        ident_f = P.tile("ident_f", [128, 128], F32)
        ident_b = P.tile("ident_b", [128, 128], BF16)
        ones_f = P.tile("ones_f", [128, 128], F32)
        P.pool(lambda e: e.memset(ones_f[:, :], 1.0), w=[ones_f])
        P.pool(lambda e: e.affine_select(out=ident_f[:, :], in_=ones_f[:, :], pattern=[[-1, 128]], compare_op=ALU.is_equal,
                                         fill=0.0, base=0, channel_multiplier=1), r=[ones_f], w=[ident_f])
        P.dve(lambda e: e.tensor_copy(out=ident_b[:, :], in_=ident_f[:, :]), r=[ident_f], w=[ident_b])

        PB = [P.ptile("pb%d" % i, [128, 512]) for i in range(7)]
        psT = P.ptile("pbT", [128, 1024], BF16)
        rr = {'A': 0, 'S': 0}

        def psA():
            rr['A'] = (rr['A'] + 1) % rr.get('An', 3)
            return PB[rr['A']]

        def psS():
            rr['S'] = (rr['S'] + 1) % 2
            return PB[3 + rr['S']]

        NWB = 4
        wbufs = [P.tile("wbuf%d" % i, [128, 4096], BF16) for i in range(NWB)]
        wrr = [0]

        def wload(name, l, r0, kc, c0, ncols):
            wrr[0] = (wrr[0] + 1) % NWB
            wt = wbufs[wrr[0]]
            view = wt[:, 0:kc * ncols].rearrange("p (k c) -> p k c", k=kc)
            src = wbd[name][l][r0:r0 + kc * 128, c0:c0 + ncols].rearrange("(k p) c -> p k c", p=128)
            P.dma('sp', view, src, r=b_wconv[(name, l)], w=[wt])
            return wt, view

        xt = P.tile("xt", [128, 4, D], F32)
        xts = [Tl(xt.t, Buf()) for _ in range(4)]
        hT = P.tile("hT", [128, 8, TT], BF16)
        mixed = P.tile("mixed", [128, 8, TT], F32) if MIX else None
        b_xd = [[Buf() for _ in range(4)] for _ in range(NT)]
        gfm = P.tile("gfm", [128, 2, 8], F32)
        ss = P.tile("ss", [128, 4], F32)
        rstd = P.tile("rstd", [128, 4], F32)

        def rmsnorm_T(which, sc):
            junk = P.tile("junk", [128, D], BF16, sc)
            xn = P.tile("xn", [128, D], BF16, sc)
            for s in range(4):
                P.act(lambda e, s=s: e.activation(out=junk[:, :], in_=xt[:, s, :], func=AF.Square, accum_out=ss[:, s:s + 1]),
                      r=[xts[s]], w=[junk, ss])
            P.act(lambda e: e.activation(out=rstd[:, :], in_=ss[:, :], func=AF.Sqrt, scale=1.0 / D, bias=epsb[:, 0:1]),
                  r=[ss, epsb], w=[rstd])
            P.dve(lambda e: e.reciprocal(out=rstd[:, :], in_=rstd[:, :]), r=[rstd], w=[rstd])
            for s in range(4):
                P.dve(lambda e, s=s: e.tensor_scalar(out=xn[:, :], in0=xt[:, s, :], scalar1=rstd[:, s:s + 1], scalar2=None,
                                                     op0=ALU.mult), r=[xts[s], rstd], w=[xn])
                for k in range(8):
                    P.pe(lambda e, k=k: e.transpose(out=psT[:, k * 128:(k + 1) * 128], in_=xn[:, k * 128:(k + 1) * 128],
                                                    identity=ident_b[:, :]), r=[xn, ident_b], w=[psT])
                P.dve(lambda e, s=s: e.tensor_tensor(
                    out=hT[:, :, s * 128:(s + 1) * 128], in0=psT[:, :].rearrange("p (k c) -> p k c", k=8),
                    in1=gfm[:, which, :].unsqueeze(2).to_broadcast([128, 8, 128]), op=ALU.mult),
                    r=[psT, gfm], w=[hT])

        def fm_mm(pt, view, cols, rhs_fn, kc, M=128, rd=()):
            for k in range(kc):
                rhs_ = rhs_fn(k)
                lt_ = view[:, k, cols[0]:cols[1]]
                P.pe(lambda e, k=k, rhs_=rhs_, lt_=lt_: e.matmul(pt[0:M, :], lhsT=lt_, rhs=rhs_,
                                                                 start=(k == 0), stop=(k == kc - 1)), r=list(rd), w=[pt])

        epsb = P.tile("epsb", [128, 2], F32)
        P.pool(lambda e: e.memset(epsb[:, :], 1e-6), w=[epsb])

        first_mix = [True] * 8
        sgp = {}
        glp = {}
        FR = 256
        if F_S:
            tabre = P.tile("tabre", [128, 16, FR], F32)
            tabim = P.tile("tabim", [128, 16, FR], F32)
            BT = P.tile("BT", [128, 16, 2, 128], BF16)
            CT = P.tile("CT", [128, 16, 2, 128], BF16)
            DT = P.tile("DT", [128, 4, 128], BF16)
            CTn = P.tile("CTn", [128, 16, 128], BF16)
            rdec = P.tile("rdec", [128, 16], F32)
            car = P.tile("car", [128, 16, 2], F32)
            halfpi = P.tile("halfpi", [128, 1], F32)
            lre = P.tile("lre", [128, 16], F32)
            lim = P.tile("lim", [128, 16], F32)
            ldt = P.tile("ldt", [128, 16], F32)
            Bsb = P.tile("Bsb", [128, 2, 16, 16], F32)
            Csb = P.tile("Csb", [128, 2, 2, 128], F32)
            dsb = P.tile("dsb", [128, 4], F32)
            P.pool(lambda e: e.memset(halfpi[:, :], float(np.pi / 2)), w=[halfpi])

        if F_N:
            WV = 65 + NBLK
            ksT = P.tile("ksT", [128, S], BF16)
            kwT = P.tile("kwT", [128, S], BF16)
            vs = P.tile("vs", [128, NKT, 2, 65], BF16)
            vw = P.tile("vw", [128, NKT, 2, 65], BF16)
            kcT = P.tile("kcT", [128, 128], BF16)
            vcT = P.tile("vcT", [128, 128], BF16)
            vcaug = P.tile("vcaug", [128, 2, WV], BF16)
            Em = P.tile("Em", [NBLK, NKT, 128], BF16)
            TA = P.tile("TA", [128, 2 * NBLK], F32)
            TB = P.tile("TB", [128, 2 * NBLK], F32)
            tri_le = P.tile("tri_le", [128, 128], BF16)
            tri_gt = P.tile("tri_gt", [128, 128], BF16)
            bd1 = P.tile("bd1", [128, 128], F32)
            gq = P.tile("gq", [128, 1], F32)
            gk = P.tile("gk", [128, 3], F32)
            peT = P.tile("peT", [128, 2, 32], F32)
            w2sb = P.tile("w2sb", [128, 2, 2, 64], BF16)
            mT = P.tile("mT", [64, 2, 128], BF16)
            P.pool(lambda e: e.memset(vs[:, :, :, :], 1.0), w=[vs])
            P.pool(lambda e: e.memset(vw[:, :, :, :], 1.0), w=[vw])
            P.pool(lambda e: e.memset(vcaug[:, :, :], 1.0), w=[vcaug])
            for h_ in range(2):
                P.pool(lambda e, h_=h_: e.affine_select(out=vcaug[:, h_, 65:WV], in_=vcaug[:, h_, 65:WV], pattern=[[-2, NBLK]],
                                                compare_op=ALU.is_ge, fill=0.0, base=0, channel_multiplier=1), r=[vcaug], w=[vcaug])
                P.pool(lambda e, h_=h_: e.affine_select(out=vcaug[:, h_, 65:WV], in_=vcaug[:, h_, 65:WV], pattern=[[2, NBLK]],
                                                compare_op=ALU.is_ge, fill=0.0, base=1, channel_multiplier=-1), r=[vcaug], w=[vcaug])
            P.pool(lambda e: e.memset(Em[:, :, :], 1.0), w=[Em])
            P.pool(lambda e: e.affine_select(out=Em[:, :, :], in_=Em[:, :, :], pattern=[[128, NKT], [1, 128]], compare_op=ALU.is_ge,
                                             fill=0.0, base=0, channel_multiplier=-64), r=[Em], w=[Em])
            P.pool(lambda e: e.affine_select(out=Em[:, :, :], in_=Em[:, :, :], pattern=[[-128, NKT], [-1, 128]], compare_op=ALU.is_ge,
                                             fill=0.0, base=63, channel_multiplier=64), r=[Em], w=[Em])
            cc_ = NBLK
            P.pool(lambda e: e.memset(TA[:, :], 1.0), w=[TA])
            P.pool(lambda e: e.memset(TA[0:64, cc_ - 1:], 0.0), w=[TA])
            P.pool(lambda e: e.memset(TA[64:128, cc_:], 0.0), w=[TA])
            P.pool(lambda e: e.memset(TB[:, :], 0.0), w=[TB])
            P.pool(lambda e: e.memset(TB[0:64, cc_ + 1:], -1e30), w=[TB])
            P.pool(lambda e: e.memset(TB[0:64, cc_ - 1:cc_ + 1], 1e4), w=[TB])
            P.pool(lambda e: e.memset(TB[64:128, cc_ + 2:], -1e30), w=[TB])
            P.pool(lambda e: e.memset(TB[64:128, cc_:cc_ + 2], 1e4), w=[TB])
            onesb = P.tile("onesb", [128, 128], BF16)
            P.pool(lambda e: e.memset(onesb[:, :], 1.0), w=[onesb])
            P.pool(lambda e: e.affine_select(out=tri_le[:, :], in_=onesb[:, :], pattern=[[1, 128]], compare_op=ALU.is_ge,
                                             fill=0.0, base=0, channel_multiplier=-1), r=[onesb], w=[tri_le])
            P.pool(lambda e: e.affine_select(out=tri_gt[:, :], in_=onesb[:, :], pattern=[[-1, 128]], compare_op=ALU.is_gt,
                                             fill=0.0, base=0, channel_multiplier=1), r=[onesb], w=[tri_gt])
            P.pool(lambda e: e.memset(bd1[:, :], 0.0), w=[bd1])
            P.pool(lambda e: e.memset(bd1[0:64, 0:64], 1.0), w=[bd1])
            P.pool(lambda e: e.memset(bd1[64:128, 64:128], 1.0), w=[bd1])

        def nsa_setup(l):
            for hh in range(2):
                rows = slice(hh * 64, hh * 64 + 64)
                P.dma('sp', gq[rows, :], prm['q_norm_g'][l].rearrange("(d o) -> d o", o=1), w=[gq])
                P.dma('sp', gk[rows, :], prm['k_norm_g'][l].rearrange("b d -> d b"), w=[gk], allow_slow_non_contiguous=True)
                for w_ in range(2):
                    P.dma('sp', peT[rows, w_, :], prm['cmp_pe'][l][w_].rearrange("l d -> d l"), w=[peT],
                          allow_slow_non_contiguous=True)
            P.dve(lambda e: e.tensor_scalar(out=gq[:, :], in0=gq[:, :], scalar1=0.125, scalar2=None, op0=ALU.mult), r=[gq], w=[gq])
            for w_ in range(2):
                P.dma('sp', w2sb[:, w_, :, :], wbd['cmp_w2'][l][w_ * 256:(w_ + 1) * 256, :].rearrange("(j p) d -> p j d", p=128),
                      r=b_wconv[('cmp_w2', l)], w=[w2sb])
            P.pool(lambda e: e.memset(kcT[:, :], 0.0), w=[kcT])
            P.pool(lambda e: e.memset(vcT[:, :], 0.0), w=[vcT])

        def s5_setup(l):
            P.new_step()
            with ExitStack() as sc:
                def sm(name, shape=(128, 16)):
                    return P.tile(name, list(shape), F32, sc)
                P.dma('sp', lre[:, :], bass.AP(prm['ssm_lam_re'], l * 2048, [[1, 128], [128, 16]]), w=[lre],
                      allow_slow_non_contiguous=True)
                P.dma('sp', lim[:, :], bass.AP(prm['ssm_lam_im'], l * 2048, [[1, 128], [128, 16]]), w=[lim],
                      allow_slow_non_contiguous=True)
                for gl in range(2):
                    P.dma('sp', ldt[gl * 64:(gl + 1) * 64, :], bass.AP(prm['ssm_log_dt'], l * 32 + gl, [[0, 64], [2, 16]]),
                          w=[ldt], allow_slow_non_contiguous=True)
                dt_ = sm("dt")
                P.act(lambda e: e.activation(out=dt_[:, :], in_=ldt[:, :], func=AF.Exp), r=[ldt], w=[dt_])
                are, th = sm("are"), sm("th")
                P.dve(lambda e: e.tensor_tensor(out=are[:, :], in0=lre[:, :], in1=dt_[:, :], op=ALU.mult), r=[lre, dt_], w=[are])
                P.dve(lambda e: e.tensor_tensor(out=th[:, :], in0=lim[:, :], in1=dt_[:, :], op=ALU.mult), r=[lim, dt_], w=[th])
                P.act(lambda e: e.activation(out=rdec[:, :], in_=are[:, :], func=AF.Exp), r=[are], w=[rdec])
                c_, s_ = sm("c"), sm("s")
                P.act(lambda e: e.activation(out=s_[:, :], in_=th[:, :], func=AF.Sin, scale=1.0 / 32), r=[th], w=[s_])
                P.act(lambda e: e.activation(out=c_[:, :], in_=th[:, :], func=AF.Sin, scale=1.0 / 32, bias=halfpi[:, 0:1]),
                      r=[th, halfpi], w=[c_])
                t1, t2, t3 = sm("t1"), sm("t2"), sm("t3")
                for _ in range(5):
                    P.dve(lambda e: e.tensor_tensor(out=t1[:, :], in0=c_[:, :], in1=c_[:, :], op=ALU.mult), r=[c_], w=[t1])
                    P.dve(lambda e: e.tensor_tensor(out=t2[:, :], in0=s_[:, :], in1=s_[:, :], op=ALU.mult), r=[s_], w=[t2])
                    P.dve(lambda e: e.tensor_tensor(out=t3[:, :], in0=c_[:, :], in1=s_[:, :], op=ALU.mult), r=[c_, s_], w=[t3])
                    P.dve(lambda e: e.tensor_tensor(out=c_[:, :], in0=t1[:, :], in1=t2[:, :], op=ALU.subtract), r=[t1, t2], w=[c_])
                    P.dve(lambda e: e.tensor_scalar(out=s_[:, :], in0=t3[:, :], scalar1=2.0, scalar2=None, op0=ALU.mult),
                          r=[t3], w=[s_])
                nre, nim, den, cre, cim = sm("nre"), sm("nim"), sm("den"), sm("cre"), sm("cim")
                P.dve(lambda e: e.tensor_tensor(out=nre[:, :], in0=rdec[:, :], in1=c_[:, :], op=ALU.mult), r=[rdec, c_], w=[nre])
                P.dve(lambda e: e.tensor_scalar(out=nre[:, :], in0=nre[:, :], scalar1=-1.0, scalar2=None, op0=ALU.add),
                      r=[nre], w=[nre])
                P.dve(lambda e: e.tensor_tensor(out=nim[:, :], in0=rdec[:, :], in1=s_[:, :], op=ALU.mult), r=[rdec, s_], w=[nim])
                P.dve(lambda e: e.tensor_tensor(out=t1[:, :], in0=lre[:, :], in1=lre[:, :], op=ALU.mult), r=[lre], w=[t1])
                P.dve(lambda e: e.tensor_tensor(out=t2[:, :], in0=lim[:, :], in1=lim[:, :], op=ALU.mult), r=[lim], w=[t2])
                P.dve(lambda e: e.tensor_tensor(out=den[:, :], in0=t1[:, :], in1=t2[:, :], op=ALU.add), r=[t1, t2], w=[den])
                P.dve(lambda e: e.reciprocal(out=den[:, :], in_=den[:, :]), r=[den], w=[den])
                P.dve(lambda e: e.tensor_tensor(out=t1[:, :], in0=nre[:, :], in1=lre[:, :], op=ALU.mult), r=[nre, lre], w=[t1])
                P.dve(lambda e: e.tensor_tensor(out=t2[:, :], in0=nim[:, :], in1=lim[:, :], op=ALU.mult), r=[nim, lim], w=[t2])
                P.dve(lambda e: e.tensor_tensor(out=t1[:, :], in0=t1[:, :], in1=t2[:, :], op=ALU.add), r=[t1, t2], w=[t1])
                P.dve(lambda e: e.tensor_tensor(out=cre[:, :], in0=t1[:, :], in1=den[:, :], op=ALU.mult), r=[t1, den], w=[cre])
                P.dve(lambda e: e.tensor_tensor(out=t1[:, :], in0=nim[:, :], in1=lre[:, :], op=ALU.mult), r=[nim, lre], w=[t1])
                P.dve(lambda e: e.tensor_tensor(out=t2[:, :], in0=nre[:, :], in1=lim[:, :], op=ALU.mult), r=[nre, lim], w=[t2])
                P.dve(lambda e: e.tensor_tensor(out=t1[:, :], in0=t1[:, :], in1=t2[:, :], op=ALU.subtract), r=[t1, t2], w=[t1])
                P.dve(lambda e: e.tensor_tensor(out=cim[:, :], in0=t1[:, :], in1=den[:, :], op=ALU.mult), r=[t1, den], w=[cim])
                P.dve(lambda e: e.tensor_copy(out=tabre[:, :, 0:1], in_=c_[:, :].unsqueeze(2)), r=[c_], w=[tabre])
                P.dve(lambda e: e.tensor_scalar(out=tabim[:, :, 0:1], in0=s_[:, :].unsqueeze(2), scalar1=-1.0, scalar2=None,
                                                op0=ALU.mult), r=[s_], w=[tabim])
                with ExitStack() as sc2:
                    tA = P.tile("tA", [128, 16, FR // 2], F32, sc2)
                    tB = P.tile("tB", [128, 16, FR // 2], F32, sc2)
                    tCf = P.tile("tC", [128, 16, FR // 2], F32, sc2)
                    n = 1
                    while n < FR:
                        def bc(t, n=n):
                            return t[:, :, n - 1:n].to_broadcast([128, 16, n])
                        P.dve(lambda e, n=n, bc=bc: e.tensor_tensor(out=tA[:, :, 0:n], in0=tabre[:, :, 0:n], in1=bc(tabre), op=ALU.mult),
                              r=[tabre], w=[tA])
                        P.dve(lambda e, n=n, bc=bc: e.tensor_tensor(out=tB[:, :, 0:n], in0=tabim[:, :, 0:n], in1=bc(tabim), op=ALU.mult),
                              r=[tabim], w=[tB])
                        P.dve(lambda e, n=n: e.tensor_tensor(out=tCf[:, :, 0:n], in0=tA[:, :, 0:n], in1=tB[:, :, 0:n], op=ALU.subtract),
                              r=[tA, tB], w=[tCf])
                        P.dve(lambda e, n=n, bc=bc: e.tensor_tensor(out=tA[:, :, 0:n], in0=tabre[:, :, 0:n], in1=bc(tabim), op=ALU.mult),
                              r=[tabre, tabim], w=[tA])
                        P.dve(lambda e, n=n, bc=bc: e.tensor_tensor(out=tB[:, :, 0:n], in0=tabim[:, :, 0:n], in1=bc(tabre), op=ALU.mult),
                              r=[tabre, tabim], w=[tB])
                        P.dve(lambda e, n=n: e.tensor_tensor(out=tabim[:, :, n:2 * n], in0=tA[:, :, 0:n], in1=tB[:, :, 0:n], op=ALU.add),
                              r=[tA, tB], w=[tabim])
                        P.dve(lambda e, n=n: e.tensor_copy(out=tabre[:, :, n:2 * n], in_=tCf[:, :, 0:n]), r=[tCf], w=[tabre])
                        n *= 2
                P.barrier()
                with ExitStack() as sc3:
                    pass
                    for ri, nm in enumerate(('ssm_b_re', 'ssm_b_im')):
                        P.dma('sp', Bsb[:, ri, :, :], bass.AP(prm[nm], l * 32768, [[16, 128], [2048, 16], [1, 16]]), w=[Bsb])
                    Z = P.tile("Z", [128, 16, 128], F32, sc3)
                    P.pool(lambda e: e.memset(Z[:, :, :], 0.0), w=[Z])
                    Zv = Z[:, :, :].rearrange("p (a b) c -> p a b c", b=4)
                    for ri in range(2):
                        Bv = Bsb[:, ri, :, :].rearrange("p (a b) h -> p a b h", b=4)
                        for b_ in range(4):
                            for gl in range(2):
                                c0 = 32 * b_ + 16 * gl
                                P.dve(lambda e, b_=b_, gl=gl, c0=c0, Bv=Bv: e.tensor_copy(
                                    out=Zv[gl * 64:(gl + 1) * 64, :, b_, c0:c0 + 16], in_=Bv[gl * 64:(gl + 1) * 64, :, b_, :]),
                                    r=[Bsb], w=[Z])
                        for j in range(16):
                            pt = psA()
                            P.pe(lambda e, j=j, pt=pt: e.transpose(out=pt[:, 0:128], in_=Z[:, j, :], identity=ident_f[:, :]),
                                 r=[Z, ident_f], w=[pt])
                            P.act(lambda e, j=j, pt=pt, ri=ri: e.activation(out=BT[:, j, ri, :], in_=pt[:, 0:128], func=AF.Copy),
                                  r=[pt], w=[BT])
                    pass
                    for ri, nm in enumerate(('ssm_c_re', 'ssm_c_im')):
                        for j in range(16):
                            P.dma('sp', Csb[(j % 8) * 16:(j % 8) * 16 + 16, j // 8, ri, :].rearrange("p (g q) -> p g q", g=2),
                                  bass.AP(prm[nm], l * 32768 + j * 2048, [[64, 16], [1024, 2], [1, 64]]), w=[Csb])
                    Cr = P.tile("Cr", [128, 2, 16, 16], F32, sc3)
                    for ri in range(2):
                        for a in range(2):
                            pt = psA()
                            P.pe(lambda e, a=a, ri=ri, pt=pt: e.transpose(out=pt[:, 0:128], in_=Csb[:, a, ri, :], identity=ident_f[:, :]),
                                 r=[Csb, ident_f], w=[pt])
                            P.act(lambda e, a=a, ri=ri, pt=pt: e.activation(
                                out=Cr[:, ri, a * 8:(a + 1) * 8, :], in_=pt[:, 0:128].rearrange("p (j h) -> p j h", h=16), func=AF.Copy),
                                r=[pt], w=[Cr])
                    Cp = P.tile("Cp", [128, 2, 16, 16], F32, sc3)
                    u1 = P.tile("u1", [128, 16, 16], F32, sc3)
                    u2 = P.tile("u2", [128, 16, 16], F32, sc3)

                    def cb(t):
                        return t[:, :].unsqueeze(2).to_broadcast([128, 16, 16])
                    P.dve(lambda e: e.tensor_tensor(out=u1[:, :, :], in0=Cr[:, 0, :, :], in1=cb(cre), op=ALU.mult), r=[Cr, cre], w=[u1])
                    P.dve(lambda e: e.tensor_tensor(out=u2[:, :, :], in0=Cr[:, 1, :, :], in1=cb(cim), op=ALU.mult), r=[Cr, cim], w=[u2])
                    P.dve(lambda e: e.tensor_tensor(out=Cp[:, 0, :, :], in0=u1[:, :, :], in1=u2[:, :, :], op=ALU.subtract),
                          r=[u1, u2], w=[Cp])
                    P.dve(lambda e: e.tensor_tensor(out=u1[:, :, :], in0=Cr[:, 0, :, :], in1=cb(cim), op=ALU.mult), r=[Cr, cim], w=[u1])
                    P.dve(lambda e: e.tensor_tensor(out=u2[:, :, :], in0=Cr[:, 1, :, :], in1=cb(cre), op=ALU.mult), r=[Cr, cre], w=[u2])
                    P.dve(lambda e: e.scalar_tensor_tensor(out=Cp[:, 1, :, :], in0=u1[:, :, :], scalar=-1.0, in1=u2[:, :, :],
                                                           op0=ALU.mult, op1=ALU.subtract), r=[u1, u2], w=[Cp])
                    P.pool(lambda e: e.memset(CT[:, :, :, :], 0.0), w=[CT])
                    CTv = CT[:, :, :, :].rearrange("p (a b) r c -> p a b r c", b=4)
                    for ri in range(2):
                        Cv = Cp[:, ri, :, :].rearrange("p (a b) h -> p a b h", b=4)
                        for b_ in range(4):
                            for gl in range(2):
                                c0 = 32 * b_ + 16 * gl
                                P.dve(lambda e, b_=b_, gl=gl, c0=c0, Cv=Cv, ri=ri: e.tensor_copy(
                                    out=CTv[gl * 64:(gl + 1) * 64, :, b_, ri, c0:c0 + 16], in_=Cv[gl * 64:(gl + 1) * 64, :, b_, :]),
                                    r=[Cp], w=[CT])
                    P.dve(lambda e: e.tensor_scalar(out=CTn[:, :, :], in0=CT[:, :, 1, :], scalar1=-1.0, scalar2=None, op0=ALU.mult),
                          r=[CT], w=[CTn])
                    P.dma('sp', dsb[:, :], prm['ssm_d'][l].rearrange("(k p) -> p k", p=128), w=[dsb], allow_slow_non_contiguous=True)
                    for k in range(4):
                        P.dve(lambda e, k=k: e.tensor_scalar(out=DT[:, k, :], in0=ident_f[:, :], scalar1=dsb[:, k:k + 1], scalar2=None,
                                                             op0=ALU.mult), r=[ident_f, dsb], w=[DT])
                P.barrier()
                P.pool(lambda e: e.memset(car[:, :, :], 0.0), w=[car])
            P.barrier()

        def gelu_to(out_ap, out_tl, src_ps, src_tl, sc, shape, temps=None):
            kk_ = (id(sc), tuple(shape))
            if temps is None and glp.get('k') != kk_:
                glp.clear()
                glp['k'] = kk_
                glp['t'] = [(P.tile("gx", shape, F32, sc), P.tile("gx2", shape, F32, sc)) for _ in range(1)]
                glp['i'] = 0
            xs, x2 = temps if temps is not None else glp['t'][glp['i']]
            pp = shape[0]
            P.act(lambda e: e.activation(out=xs[:, :], in_=src_ps, func=AF.Copy), r=[src_tl], w=[xs])
            P.act(lambda e: e.activation(out=x2[:, :], in_=src_ps, func=AF.Square), r=[src_tl], w=[x2])
            P.dve(lambda e: e.tensor_scalar(out=x2[:, :], in0=x2[:, :], scalar1=0.044715, scalar2=1.0, op0=ALU.mult, op1=ALU.add),
                  r=[x2], w=[x2])
            P.dve(lambda e: e.tensor_tensor(out=x2[:, :], in0=x2[:, :], in1=xs[:, :], op=ALU.mult), r=[x2, xs], w=[x2])
            P.act(lambda e: e.activation(out=x2[:, :], in_=x2[:, :], func=AF.Sigmoid, scale=GELU_K), r=[x2], w=[x2])
            P.dve(lambda e: e.tensor_tensor(out=out_ap, in0=x2[:, :], in1=xs[:, :], op=ALU.mult), r=[x2, xs], w=[out_tl])
        if F_C:
            cw = P.tile("cw", [128, 3, 4], F32)
            halo = P.tile("halo", [128, 4, 2], F32)

        def acc_mixed(k, src_ap, src_tl, sc, l, gate_view_fn, branch):
            wt, gv = gate_view_fn(k)
            pg = psA()
            fm_mm(pg, gv, ((k % 4) * 128, (k % 4) * 128 + 128), lambda kk: hT[:, kk, :], 8, rd=[wt, hT])
            kk_ = id(sc)
            if sgp.get('k') != kk_:
                sgp.clear()
                sgp['k'] = kk_
                sgp['t'] = [P.tile("sg", [128, TT], F32, sc) for _ in range(2)]
                sgp['i'] = 0
            sgp['i'] ^= 1
            sg = sgp['t'][sgp['i']]
            P.act(lambda e: e.activation(out=sg[:, :], in_=pg[:, :], func=AF.Sigmoid), r=[pg], w=[sg])
            if first_mix[k]:
                P.dve(lambda e: e.tensor_tensor(out=mixed[:, k, :], in0=src_ap, in1=sg[:, :], op=ALU.mult),
                      r=[src_tl, sg], w=[mixed])
                first_mix[k] = False
            else:
                P.dve(lambda e: e.tensor_tensor(out=sg[:, :], in0=src_ap, in1=sg[:, :], op=ALU.mult),
                      r=[src_tl, sg], w=[sg])
                P.pool(lambda e: e.tensor_tensor(out=mixed[:, k, :], in0=mixed[:, k, :], in1=sg[:, :], op=ALU.add),
                       r=[mixed, sg], w=[mixed])

        for l in range(depth):
            P.dma('sp', gfm[:, 0, :], prm['mix_norm_g'][l].rearrange("(k p) -> p k", p=128), w=[gfm],
                  allow_slow_non_contiguous=True)
            P.dma('sp', gfm[:, 1, :], prm['ffn_norm_g'][l].rearrange("(k p) -> p k", p=128), w=[gfm],
                  allow_slow_non_contiguous=True)
            if F_S:
                s5_setup(l)
            if F_N:
                nsa_setup(l)
            if F_C:
                for t_ in range(3):
                    P.dma('sp', cw[:, t_, :], prm['conv_w'][l][t_].rearrange("(k p) -> p k", p=128), w=[cw],
                          allow_slow_non_contiguous=True)
                P.pool(lambda e: e.memset(halo[:, :, :], 0.0), w=[halo])
            def step(l, i):
                P.new_step()
                rr['An'] = 3
                rr['A'] = 0
                t0 = i * TT
                src = x_d if l == 0 else out_d
                for s_ in range(4):
                    P.dma('sp', xt[:, s_, :], src[t0 + s_ * 128:t0 + (s_ + 1) * 128, :], r=[b_xd[i][s_]], w=[xts[s_]])
                if MIX:
                    for k in range(8):
                        first_mix[k] = True
                    with ExitStack() as sc:
                        rmsnorm_T(0, sc)
                    P.barrier()
                    gate_cache = {}

                    def gate_view(branch):
                        def f(k):
                            key = (branch, k // 4)
                            if key not in gate_cache:
                                gate_cache.clear()
                                gate_cache[key] = wload('w_in', l, 0, 8, C_MG + branch * D + (k // 4) * 512, 512)
                            return gate_cache[key]
                        return f
                    if F_S:
                        with ExitStack() as sc:
                            uT = P.tile("uT", [128, 4, TT], BF16, sc)
                            yact = P.tile("yact", [128, 4, TT], BF16, sc)
                            wt, wv = wload('w_in', l, 0, 8, C_U, 512)
                            for k in range(4):
                                pt = psA()
                                fm_mm(pt, wv, (k * 128, k * 128 + 128), lambda kk: hT[:, kk, :], 8, rd=[wt, hT])
                                P.act(lambda e, k=k, pt=pt: e.activation(out=uT[:, k, :], in_=pt[:, :], func=AF.Copy),
                                      r=[pt], w=[uT])
                            with ExitStack() as scw:
                                WS = []
                                for _ in range(2):
                                    WS.append(dict(
                                        t=[P.tile("s5t", [128, FR], F32, scw) for _ in range(4)],
                                        wre=P.tile("wre", [128, FR], F32, scw), wim=P.tile("wim", [128, FR], F32, scw),
                                        gre=P.tile("gre", [128, FR], F32, scw), gim=P.tile("gim", [128, FR], F32, scw),
                                        b=[P.tile("s5b", [128, FR], BF16, scw) for _ in range(4)], cx=P.tile("cx", [128, 2], F32, scw)))
                                carj = [Tl(car.t, Buf()) for _ in range(16)]
                                for cj in carj:
                                    cj.b.w = car.b.w

                                def s5_chain(j, k, f, ws, yacc, first, bp):
                                    t1, t2, t3, t4 = ws['t']
                                    wre, wim, gre, gim = ws['wre'], ws['wim'], ws['gre'], ws['gim']
                                    b1, b2, b3, b4 = ws['b']
                                    cx = ws['cx']
                                    cj = carj[j]
                                    c0, c1 = f * FR, (f + 1) * FR
                                    bur, bui = bp
                                    P.pe(lambda e: e.matmul(bur[:, 0:FR], lhsT=BT[:, j, 0, :], rhs=uT[:, k, c0:c1], start=True, stop=True),
                                         r=[BT, uT], w=[bur])
                                    P.pe(lambda e: e.matmul(bui[:, 0:FR], lhsT=BT[:, j, 1, :], rhs=uT[:, k, c0:c1], start=True, stop=True),
                                         r=[BT, uT], w=[bui])
                                    yield
                                    tre = tabre[:, j, :]
                                    tim = tabim[:, j, :]
                                    P.dve(lambda e: e.tensor_tensor(out=t1[:, :], in0=bur[:, 0:FR], in1=tre, op=ALU.mult), r=[bur, tabre], w=[t1])
                                    P.dve(lambda e: e.tensor_tensor(out=t2[:, :], in0=bui[:, 0:FR], in1=tim, op=ALU.mult), r=[bui, tabim], w=[t2])
                                    yield
                                    P.pool(lambda e: e.tensor_tensor(out=wre[:, :], in0=t1[:, :], in1=t2[:, :], op=ALU.subtract), r=[t1, t2], w=[wre])
                                    P.dve(lambda e: e.tensor_tensor(out=t3[:, :], in0=bui[:, 0:FR], in1=tre, op=ALU.mult), r=[bui, tabre], w=[t3])
                                    P.dve(lambda e: e.tensor_tensor(out=t4[:, :], in0=bur[:, 0:FR], in1=tim, op=ALU.mult), r=[bur, tabim], w=[t4])
                                    yield
                                    P.pool(lambda e: e.tensor_tensor(out=wim[:, :], in0=t3[:, :], in1=t4[:, :], op=ALU.add), r=[t3, t4], w=[wim])
                                    dec = rdec[:, j:j + 1].to_broadcast([128, FR])
                                    P.dve(lambda e: e.tensor_tensor_scan(out=gre[:, :], data0=dec, data1=wre[:, :], initial=car[:, j, 0:1],
                                                                         op0=ALU.mult, op1=ALU.add), r=[rdec, wre, cj], w=[gre])
                                    yield
                                    P.dve(lambda e: e.tensor_tensor_scan(out=gim[:, :], data0=dec, data1=wim[:, :], initial=car[:, j, 1:2],
                                                                         op0=ALU.mult, op1=ALU.add), r=[rdec, wim, cj], w=[gim])
                                    P.dve(lambda e: e.tensor_tensor(out=b1[:, :], in0=gre[:, :], in1=tre, op=ALU.mult), r=[gre, tabre], w=[b1])
                                    yield
                                    P.dve(lambda e: e.tensor_tensor(out=b2[:, :], in0=gim[:, :], in1=tim, op=ALU.mult), r=[gim, tabim], w=[b2])
                                    P.dve(lambda e: e.tensor_tensor(out=b3[:, :], in0=gim[:, :], in1=tre, op=ALU.mult), r=[gim, tabre], w=[b3])
                                    P.pool(lambda e: e.tensor_tensor(out=cx[:, 0:1], in0=gim[:, FR - 1:FR], in1=tabim[:, j, FR - 1:FR], op=ALU.mult),
                                           r=[gim, tabim], w=[cx])
                                    yield
                                    P.dve(lambda e: e.tensor_tensor(out=b4[:, :], in0=gre[:, :], in1=tim, op=ALU.mult), r=[gre, tabim], w=[b4])
                                    P.pool(lambda e: e.tensor_tensor(out=cx[:, 1:2], in0=gre[:, FR - 1:FR], in1=tabim[:, j, FR - 1:FR], op=ALU.mult),
                                           r=[gre, tabim], w=[cx])
                                    yield
                                    P.dve(lambda e: e.scalar_tensor_tensor(out=car[:, j, 0:1], in0=gre[:, FR - 1:FR], scalar=tabre[:, j, FR - 1:FR],
                                                                            in1=cx[:, 0:1], op0=ALU.mult, op1=ALU.add), r=[gre, tabre, cx, gim], w=[cj])
                                    P.dve(lambda e: e.scalar_tensor_tensor(out=car[:, j, 1:2], in0=gim[:, FR - 1:FR], scalar=tabre[:, j, FR - 1:FR],
                                                                            in1=cx[:, 1:2], op0=ALU.mult, op1=ALU.subtract), r=[gim, tabre, cx, gre], w=[cj])
                                    yield
                                    P.pe(lambda e: e.matmul(yacc[:, c0:c1], lhsT=CT[:, j, 0, :], rhs=b1[:, :], start=first, stop=False,
                                                            skip_group_check=True), r=[CT, b1], w=[yacc])
                                    P.pe(lambda e: e.matmul(yacc[:, c0:c1], lhsT=CT[:, j, 0, :], rhs=b2[:, :], start=False, stop=False,
                                                            skip_group_check=True), r=[CT, b2], w=[yacc])
                                    P.pe(lambda e: e.matmul(yacc[:, c0:c1], lhsT=CT[:, j, 1, :], rhs=b3[:, :], start=False, stop=False,
                                                            skip_group_check=True), r=[CT, b3], w=[yacc])
                                    P.pe(lambda e: e.matmul(yacc[:, c0:c1], lhsT=CTn[:, j, :], rhs=b4[:, :], start=False, stop=False,
                                                            skip_group_check=True), r=[CTn, b4], w=[yacc])
                                    yield

                                rr['An'] = 2
                                rr['A'] = 0
                                bps = [(PB[1], PB[2]), (PB[3], PB[4])]
                                rr['An'] = 1
                                rr['A'] = 0
                                for k in range(4):
                                    yacc = PB[5 + k % 2]
                                    for f in range(2):
                                        for pr in range(2):
                                            ga = s5_chain(4 * k + 2 * pr, k, f, WS[0], yacc, (f == 0 and pr == 0), bps[0])
                                            gb = s5_chain(4 * k + 2 * pr + 1, k, f, WS[1], yacc, False, bps[1])
                                            alive = [ga, gb]
                                            while alive:
                                                for g_ in list(alive):
                                                    if next(g_, 'done') == 'done':
                                                        alive.remove(g_)
                                    for f in range(2):
                                        P.pe(lambda e, k=k, f=f, yacc=yacc: e.matmul(yacc[:, f * FR:(f + 1) * FR], lhsT=DT[:, k, :],
                                                                                   rhs=uT[:, k, f * FR:(f + 1) * FR], start=False, stop=True,
                                                                                   skip_group_check=True), r=[DT, uT], w=[yacc])
                                    for f in range(2):
                                        gelu_to(yact[:, k, f * FR:(f + 1) * FR], yact, yacc[:, f * FR:(f + 1) * FR], yacc, scw, [128, FR],
                                                temps=(WS[0]['t'][0], WS[0]['t'][1]))
                                rr['An'] = 3
                            P.barrier()
                            wtl, wvl = wload('ssm_w_glu', l, 0, 4, 0, 1024)
                            wtg, wvg = wload('ssm_w_glu', l, 0, 4, 1024, 1024)
                            gv = gate_view(0)
                            ys = [P.tile("ys", [128, TT], F32, sc) for _ in range(1)]
                            for k in range(8):
                                pl = psA()
                                fm_mm(pl, wvl, (k * 128, k * 128 + 128), lambda kk: yact[:, kk, :], 4, rd=[wtl, yact])
                                pg_ = psA()
                                fm_mm(pg_, wvg, (k * 128, k * 128 + 128), lambda kk: yact[:, kk, :], 4, rd=[wtg, yact])
                                y_ = ys[0]
                                P.act(lambda e, pg_=pg_, y_=y_: e.activation(out=y_[:, :], in_=pg_[:, :], func=AF.Sigmoid), r=[pg_], w=[y_])
                                P.dve(lambda e, pl=pl, y_=y_: e.tensor_tensor(out=y_[:, :], in0=pl[:, :], in1=y_[:, :], op=ALU.mult),
                                      r=[pl, y_], w=[y_])
                                acc_mixed(k, y_[:, :], y_, sc, l, gv, 0)
                        P.barrier()
                    if F_C:
                        rr['An'] = 7
                        with ExitStack() as sc:
                            ccs = P.tile("ccs", [128, 4, TT], F32, sc)
                            zb = P.tile("zb", [128, 4, TT + 2], F32, sc)
                            cbz = P.tile("cbz", [128, 4, TT], BF16, sc)
                            P.pool(lambda e: e.tensor_copy(out=zb[:, :, 0:2], in_=halo[:, :, :]), r=[halo], w=[zb])
                            wt, wv = wload('w_in', l, 0, 8, C_CC, 512)
                            for k in range(4):
                                pt = psA()
                                fm_mm(pt, wv, (k * 128, k * 128 + 128), lambda kk: hT[:, kk, :], 8, rd=[wt, hT])
                                P.act(lambda e, k=k, pt=pt: e.activation(out=ccs[:, k, :], in_=pt[:, :], func=AF.Copy),
                                      r=[pt], w=[ccs])
                            wt, wv = wload('w_in', l, 0, 8, C_CX, 512)
                            for k in range(4):
                                pt = psA()
                                fm_mm(pt, wv, (k * 128, k * 128 + 128), lambda kk: hT[:, kk, :], 8, rd=[wt, hT])
                                P.dve(lambda e, k=k, pt=pt: e.tensor_tensor(out=zb[:, k, 2:TT + 2], in0=pt[:, :],
                                                                            in1=ccs[:, k, :], op=ALU.mult),
                                      r=[pt, ccs], w=[zb])
                            for k in range(4):
                                P.dve(lambda e, k=k: e.tensor_scalar(out=ccs[:, k, :], in0=zb[:, k, 2:TT + 2],
                                                                     scalar1=cw[:, 2, k:k + 1], scalar2=None, op0=ALU.mult),
                                      r=[zb, cw], w=[ccs])
                                P.dve(lambda e, k=k: e.scalar_tensor_tensor(out=ccs[:, k, :], in0=zb[:, k, 1:TT + 1],
                                                                            scalar=cw[:, 1, k:k + 1], in1=ccs[:, k, :],
                                                                            op0=ALU.mult, op1=ALU.add),
                                      r=[zb, cw, ccs], w=[ccs])
                                P.dve(lambda e, k=k: e.scalar_tensor_tensor(out=ccs[:, k, :], in0=zb[:, k, 0:TT],
                                                                            scalar=cw[:, 0, k:k + 1], in1=ccs[:, k, :],
                                                                            op0=ALU.mult, op1=ALU.add),
                                      r=[zb, cw, ccs], w=[ccs])
                            P.pool(lambda e: e.tensor_copy(out=halo[:, :, :], in_=zb[:, :, TT:TT + 2]), r=[zb], w=[halo])
                            wt, wv = wload('w_in', l, 0, 8, C_CB, 512)
                            for k in range(4):
                                pt = psA()
                                fm_mm(pt, wv, (k * 128, k * 128 + 128), lambda kk: hT[:, kk, :], 8, rd=[wt, hT])
                                P.dve(lambda e, k=k, pt=pt: e.tensor_tensor(out=cbz[:, k, :], in0=pt[:, :],
                                                                            in1=ccs[:, k, :], op=ALU.mult),
                                      r=[pt, ccs], w=[cbz])
                            wto, wvo = wload('conv_w_out', l, 0, 4, 0, 1024)
                            gv = gate_view(1)
                            for k in range(8):
                                pt = psA()
                                fm_mm(pt, wvo, (k * 128, k * 128 + 128), lambda kk: cbz[:, kk, :], 4, rd=[wto, cbz])
                                acc_mixed(k, pt[:, :], pt, sc, l, gv, 1)
                        P.barrier()
                    rr['An'] = 3
                    if F_N:
                        rr['An'] = 7
                        with ExitStack() as sc:
                            qTn = P.tile("qTn", [128, 4, TT], BF16, sc)
                            gsb = P.tile("gsb", [128, 4, 24], F32, sc)
                            scp = ExitStack()
                            nq = [P.tile("nq", [128, TT], F32, scp) for _ in range(2)]
                            ns = [P.tile("nsq", [128, TT], F32, scp) for _ in range(2)]
                            nrr = [0]

                            def headnorm(src_tl, ncols, gcol, out_ap, out_tl, rows=slice(0, 128)):
                                nrr[0] ^= 1
                                qs, sq = nq[nrr[0]], ns[nrr[0]]
                                P.act(lambda e: e.activation(out=qs[:, 0:ncols], in_=src_tl[:, 0:ncols], func=AF.Copy), r=[src_tl], w=[qs])
                                P.act(lambda e: e.activation(out=sq[:, 0:ncols], in_=src_tl[:, 0:ncols], func=AF.Square), r=[src_tl], w=[sq])
                                pss = psA()
                                P.pe(lambda e: e.matmul(pss[:, 0:ncols], lhsT=bd1[:, :], rhs=sq[:, 0:ncols], start=True, stop=True),
                                     r=[bd1, sq], w=[pss])
                                P.act(lambda e: e.activation(out=sq[:, 0:ncols], in_=pss[:, 0:ncols], func=AF.Sqrt, scale=1.0 / 64,
                                                             bias=epsb[:, 0:1]), r=[pss, epsb], w=[sq])
                                P.dve(lambda e: e.reciprocal(out=sq[:, 0:ncols], in_=sq[:, 0:ncols]), r=[sq], w=[sq])
                                P.dve(lambda e: e.scalar_tensor_tensor(out=out_ap, in0=qs[rows, 0:ncols], scalar=gcol, in1=sq[rows, 0:ncols],
                                                                       op0=ALU.mult, op1=ALU.mult), r=[qs, sq, gq, gk], w=[out_tl])

                            wt, wv = wload('w_in', l, 0, 8, C_Q, 512)
                            for g in range(4):
                                pt = psA()
                                for two in range(2):
                                    for k in range(8):
                                        c0 = two * 256 + g * 64
                                        P.pe(lambda e, k=k, c0=c0, two=two, pt=pt, wv=wv: e.matmul(
                                            pt[two * 64:(two + 1) * 64, :], lhsT=wv[:, k, c0:c0 + 64],
                                            rhs=hT[:, k, :], start=(k == 0), stop=(k == 7)), r=[wt, hT], w=[pt])
                                headnorm(pt, TT, gq[:, 0:1], qTn[:, g, :], qTn)
                            wt, wv = wload('w_in', l, 0, 8, C_KC, 512)
                            kcin = P.tile("kcin", [128, 16, 16], BF16, scp)
                            hid = P.tile("hid", [128, 2, 16], BF16, scp)
                            for w_ in range(2):
                                wt1, wv1 = wload('cmp_w1', l, w_ * 2048, 16, 0, 256)
                                for hh in range(2):
                                    pt = psA()
                                    cbase = w_ * 128 + hh * 64
                                    for two in range(2):
                                        for k in range(8):
                                            P.pe(lambda e, k=k, two=two, pt=pt, wv=wv, cbase=cbase: e.matmul(
                                                pt[two * 64:(two + 1) * 64, :], lhsT=wv[:, k, cbase:cbase + 64],
                                                rhs=hT[:, k, :], start=(k == 0), stop=(k == 7)), r=[wt, hT], w=[pt])
                                    for ph in range(2):
                                        rws = slice(ph * 64, ph * 64 + 64)
                                        P.dve(lambda e, ph=ph, rws=rws, pt=pt, w_=w_: e.tensor_tensor(
                                            out=kcin[rws, :, :],
                                            in0=pt[rws, :].rearrange("p (c l two) -> p c l two", c=16, l=16, two=2)[:, :, :, ph],
                                            in1=peT[rws, w_, :].rearrange("p (l two) -> p l two", two=2)[:, :, ph].unsqueeze(1).to_broadcast([64, 16, 16]),
                                            op=ALU.add), r=[pt, peT], w=[kcin])
                                    for jh in range(2):
                                        ph_ = psA()
                                        for l2 in range(16):
                                            P.pe(lambda e, l2=l2, jh=jh, ph_=ph_, wv1=wv1: e.matmul(
                                                ph_[:, 0:16], lhsT=wv1[:, l2, jh * 128:(jh + 1) * 128], rhs=kcin[:, :, l2],
                                                start=(l2 == 0), stop=(l2 == 15)), r=[wt1, kcin], w=[ph_])
                                        gelu_to(hid[:, jh, :], hid, ph_[:, 0:16], ph_, scp, [128, 16])
                                    pk = psA()
                                    for two in range(2):
                                        for jh in range(2):
                                            P.pe(lambda e, jh=jh, two=two, pk=pk, w_=w_: e.matmul(
                                                pk[two * 64:(two + 1) * 64, 0:16], lhsT=w2sb[:, w_, jh, :],
                                                rhs=hid[:, jh, :], start=(jh == 0), stop=(jh == 1)), r=[w2sb, hid], w=[pk])
                                    rws = slice(hh * 64, hh * 64 + 64)
                                    if w_ == 0:
                                        headnorm(pk, 16, gk[rws, 0:1], kcT[rws, i * 16:(i + 1) * 16], kcT, rows=rws)
                                    else:
                                        P.act(lambda e, pk=pk, rws=rws, i=i: e.activation(out=vcT[rws, i * 16:(i + 1) * 16], in_=pk[rws, 0:16],
                                                                                 func=AF.Copy), r=[pk], w=[vcT])
                            P.pe(lambda e: e.transpose(out=psT[:, 0:128], in_=vcT[:, :], identity=ident_b[:, :]), r=[vcT, ident_b], w=[psT])
                            P.dve(lambda e: e.tensor_copy(out=vcaug[:, :, 0:64], in_=psT[:, 0:128].rearrange("p (h d) -> p h d", h=2)),
                                  r=[psT], w=[vcaug])
                            pt = psA()
                            fm_mm(pt, wv, (256, 384), lambda kk: hT[:, kk, :], 8, rd=[wt, hT])
                            headnorm(pt, TT, gk[:, 1:2], ksT[:, t0:t0 + TT], ksT)
                            for s in range(4):
                                pt = psA()
                                for k in range(8):
                                    P.pe(lambda e, k=k, s=s, pt=pt, wv=wv: e.matmul(pt[:, 0:128], lhsT=hT[:, k, s * 128:(s + 1) * 128],
                                                                                   rhs=wv[:, k, 384:512], start=(k == 0), stop=(k == 7)),
                                         r=[wt, hT], w=[pt])
                                P.act(lambda e, s=s, pt=pt, i=i: e.activation(out=vs[:, i * 4 + s, :, 0:64],
                                                                         in_=pt[:, 0:128].rearrange("p (h d) -> p h d", h=2), func=AF.Copy),
                                      r=[pt], w=[vs])
                            wt, wv = wload('w_in', l, 0, 8, C_KW, 280)
                            pt = psA()
                            fm_mm(pt, wv, (0, 128), lambda kk: hT[:, kk, :], 8, rd=[wt, hT])
                            headnorm(pt, TT, gk[:, 2:3], kwT[:, t0:t0 + TT], kwT)
                            for s in range(4):
                                pt = psA()
                                for k in range(8):
                                    P.pe(lambda e, k=k, s=s, pt=pt, wv=wv: e.matmul(pt[:, 0:152], lhsT=hT[:, k, s * 128:(s + 1) * 128],
                                                                                   rhs=wv[:, k, 128:280], start=(k == 0), stop=(k == 7)),
                                         r=[wt, hT], w=[pt])
                                P.act(lambda e, s=s, pt=pt, i=i: e.activation(out=vw[:, i * 4 + s, :, 0:64],
                                                                         in_=pt[:, 0:128].rearrange("p (h d) -> p h d", h=2), func=AF.Copy),
                                      r=[pt], w=[vw])
                                P.act(lambda e, s=s, pt=pt: e.activation(out=gsb[:, s, :], in_=pt[:, 128:152], func=AF.Sigmoid),
                                      r=[pt], w=[gsb])
                            scp.close()
                            P.barrier()
                            oT = P.tile("oT", [128, 4, TT], BF16, sc)
                            o_accs = [P.tile("o_acc", [128, 8, 64], F32, sc) for _ in range(2)]
                            o_bf = P.tile("o_bf", [128, 512], BF16, sc)
                            eb = [P.tile("eb", [128, 512], BF16, sc) for _ in range(3)]
                            pb = [P.tile("pb", [128, 512], BF16, sc) for _ in range(6)]
                            mtris = [P.tile("mtri", [128, 128], BF16, sc) for _ in range(2)]
                            mTa = [P.tile("mTa", [64, 128], BF16, sc) for _ in range(2)]
                            den = P.tile("den", [128, 4], F32, sc)
                            scl = P.tile("scl", [128, 4], F32, sc)
                            den_c = P.tile("den_c", [128, 4], F32, sc)
                            scl_c = P.tile("scl_c", [128, 4], F32, sc)
                            imp = P.tile("imp", [128, NBLK], F32, sc)
                            imp3 = P.tile("imp3", [128, NBLK], F32, sc)
                            m8 = P.tile("m8", [128, 8], F32, sc)
                            mskb = P.tile("mskb", [128, NBLK], BF16, sc)
                            rre = {'e': 0, 'p': 0, 'm': 0}

                            def nxt_e():
                                rre['e'] = (rre['e'] + 1) % len(eb)
                                return eb[rre['e']]

                            def nxt_p():
                                rre['p'] = (rre['p'] + 1) % len(pb)
                                return pb[rre['p']]

                            def v4(t):
                                return t[:, :].rearrange("p (g q) -> p g q", g=4)

                            def finalize(Ot, h, s, branch):
                                oa = o_accs[s % 2]
                                Ov = Ot[:, 0:260].rearrange("p (g w) -> p g w", g=4)
                                P.dve(lambda e: e.tensor_scalar(out=den[:, :], in0=Ov[:, :, 64], scalar1=1e-30, scalar2=None, op0=ALU.max),
                                      r=[Ot], w=[den])
                                P.dve(lambda e: e.reciprocal(out=den[:, :], in_=den[:, :]), r=[den], w=[den])
                                gsel = gsb[:, s, :].rearrange("p (hd b) -> p hd b", b=3)[:, h * 4:(h + 1) * 4, branch]
                                P.dve(lambda e: e.tensor_tensor(out=scl[:, :], in0=den[:, :], in1=gsel, op=ALU.mult), r=[den, gsb], w=[scl])
                                for g in range(4):
                                    hd = h * 4 + g
                                    P.dve(lambda e, g=g, hd=hd: e.scalar_tensor_tensor(out=oa[:, hd, :], in0=Ov[:, g, 0:64], scalar=scl[:, g:g + 1],
                                                                                       in1=oa[:, hd, :], op0=ALU.mult, op1=ALU.add),
                                          r=[Ot, scl, oa], w=[oa])

                            def cmp_chain(s, h):
                                qi = i * 4 + s
                                tq = qi * 128
                                oa = o_accs[s % 2]
                                rows = slice(h * 64, h * 64 + 64)
                                Q = qTn[rows, :, s * 128:(s + 1) * 128]
                                sT = psS()
                                P.pe(lambda e: e.matmul(v4(sT), lhsT=kcT[rows, :], rhs=Q, start=True, stop=True), r=[kcT, qTn], w=[sT])
                                e_ = nxt_e()
                                P.act(lambda e: e.activation(out=e_[:, :], in_=sT[:, :], func=AF.Exp), r=[sT], w=[e_])
                                p_ = nxt_p()
                                P.pool(lambda e: e.affine_select(out=v4(p_), in_=v4(e_), pattern=[[0, 4], [1, 128]], compare_op=ALU.is_ge,
                                                                 fill=0.0, base=tq - 31, channel_multiplier=-32), r=[e_], w=[p_])
                                yield
                                Ot = PB[2]
                                gsel = gsb[:, s, :].rearrange("p (hd b) -> p hd b", b=3)[:, h * 4:(h + 1) * 4, 0]
                                for hf in range(2):
                                    for g2 in range(2):
                                        g = hf * 2 + g2
                                        P.pe(lambda e, g=g, g2=g2: e.matmul(Ot[:, g2 * WV:(g2 + 1) * WV], lhsT=p_[:, g * 128:(g + 1) * 128],
                                                                           rhs=vcaug[:, h, :], start=(g2 == 0), stop=True, skip_group_check=True),
                                             r=[p_, vcaug], w=[Ot])
                                    yield
                                    P.dve(lambda e, hf=hf: e.tensor_scalar(
                                        out=den_c[:, 2 * hf:2 * hf + 2], in0=Ot[:, 0:2 * WV].rearrange("p (g w) -> p g w", g=2)[:, :, 64],
                                        scalar1=1e-30, scalar2=None, op0=ALU.max), r=[Ot], w=[den_c])
                                    P.dve(lambda e, hf=hf: e.reciprocal(out=den_c[:, 2 * hf:2 * hf + 2], in_=den_c[:, 2 * hf:2 * hf + 2]),
                                          r=[den_c], w=[den_c])
                                    P.dve(lambda e, hf=hf: e.tensor_tensor(out=scl_c[:, 2 * hf:2 * hf + 2], in0=den_c[:, 2 * hf:2 * hf + 2],
                                                                           in1=gsel[:, 2 * hf:2 * hf + 2], op=ALU.mult), r=[den_c, gsb], w=[scl_c])
                                    yield
                                    for g2 in range(2):
                                        g = hf * 2 + g2
                                        o0 = g2 * WV
                                        hd = h * 4 + g
                                        P.dve(lambda e, g=g, hd=hd, o0=o0: e.tensor_scalar(
                                            out=oa[:, hd, :], in0=Ot[:, o0:o0 + 64], scalar1=scl_c[:, g:g + 1], scalar2=None, op0=ALU.mult),
                                            r=[Ot, scl_c], w=[oa])
                                        if g == 0:
                                            P.dve(lambda e, o0=o0: e.tensor_scalar(
                                                out=imp[:, :], in0=Ot[:, o0 + 65:o0 + WV], scalar1=den_c[:, 0:1], scalar2=None, op0=ALU.mult),
                                                r=[Ot, den_c], w=[imp])
                                        else:
                                            P.dve(lambda e, g=g, o0=o0: e.scalar_tensor_tensor(
                                                out=imp[:, :], in0=Ot[:, o0 + 65:o0 + WV], scalar=den_c[:, g:g + 1], in1=imp[:, :],
                                                op0=ALU.mult, op1=ALU.add), r=[Ot, den_c, imp], w=[imp])
                                    yield
                                a0 = NBLK - 2 * qi
                                P.dve(lambda e: e.tensor_tensor(out=imp[:, :], in0=imp[:, :], in1=TA[:, a0:a0 + NBLK], op=ALU.mult),
                                      r=[imp, TA], w=[imp])
                                P.dve(lambda e: e.tensor_tensor(out=imp[:, :], in0=imp[:, :], in1=TB[:, a0:a0 + NBLK], op=ALU.add),
                                      r=[imp, TB], w=[imp])
                                P.dve(lambda e: e.memset(imp[:, 0:1], 1e4), r=[], w=[imp])
                                yield
                                P.dve(lambda e: e.max(out=m8[:, :], in_=imp[:, :]), r=[imp], w=[m8])
                                P.dve(lambda e: e.match_replace(out=imp3[:, :], in_to_replace=m8[:, :], in_values=imp[:, :], imm_value=-3e38),
                                      r=[imp, m8], w=[imp3])
                                yield
                                P.dve(lambda e: e.max(out=m8[:, :], in_=imp3[:, :]), r=[imp3], w=[m8])
                                P.dve(lambda e: e.tensor_scalar(out=mskb[:, :], in0=imp[:, :], scalar1=m8[:, 7:8], scalar2=None, op0=ALU.is_ge),
                                      r=[imp, m8], w=[mskb])
                                yield
                                P.pe(lambda e: e.transpose(out=psT[0:NBLK, 0:128], in_=mskb[:, :], identity=ident_b[:, :]),
                                     r=[mskb, ident_b], w=[psT])
                                P.dve(lambda e: e.tensor_copy(out=mTa[h][0:NBLK, :], in_=psT[0:NBLK, 0:128]), r=[psT], w=[mTa[h]])
                                yield

                            def stage_a(kind, kt, s, h):
                                qi = i * 4 + s
                                rows = slice(h * 64, h * 64 + 64)
                                Q = qTn[rows, :, s * 128:(s + 1) * 128]
                                sT = psS()
                                if kind == 'sel':
                                    P.pe(lambda e: e.matmul(v4(sT), lhsT=ksT[rows, kt * 128:(kt + 1) * 128], rhs=Q, start=True, stop=True),
                                         r=[ksT, qTn], w=[sT])
                                    mx = psA()
                                    P.pe(lambda e: e.matmul(mx[:, 0:128], lhsT=Em[:, kt, :], rhs=mTa[h][0:NBLK, :], start=True, stop=True),
                                         r=[Em, mTa[h]], w=[mx])
                                    e_ = nxt_e()
                                    P.act(lambda e: e.activation(out=e_[:, :], in_=sT[:, :], func=AF.Exp), r=[sT], w=[e_])
                                    p_ = nxt_p()
                                    if kt == qi:
                                        rre['m'] ^= 1
                                        mtri = mtris[rre['m']]
                                        P.dve(lambda e: e.tensor_tensor(out=mtri[:, :], in0=mx[:, 0:128], in1=tri_le[:, :], op=ALU.mult),
                                              r=[mx, tri_le], w=[mtri])
                                        P.pool(lambda e: e.tensor_tensor(out=v4(p_), in0=v4(e_), in1=mtri[:, :].unsqueeze(1).to_broadcast([128, 4, 128]),
                                                                         op=ALU.mult), r=[e_, mtri], w=[p_])
                                    else:
                                        P.dve(lambda e: e.tensor_tensor(out=v4(p_), in0=v4(e_), in1=mx[:, 0:128].unsqueeze(1).to_broadcast([128, 4, 128]),
                                                                        op=ALU.mult), r=[e_, mx], w=[p_])
                                    return p_
                                P.pe(lambda e: e.matmul(v4(sT), lhsT=kwT[rows, kt * 128:(kt + 1) * 128], rhs=Q, start=True, stop=True),
                                     r=[kwT, qTn], w=[sT])
                                p_ = nxt_p()
                                if kt == qi or kt == qi - 4:
                                    msk_ = tri_le if kt == qi else tri_gt
                                    e_ = nxt_e()
                                    P.act(lambda e: e.activation(out=e_[:, :], in_=sT[:, :], func=AF.Exp), r=[sT], w=[e_])
                                    P.pool(lambda e: e.tensor_tensor(out=v4(p_), in0=v4(e_), in1=msk_[:, :].unsqueeze(1).to_broadcast([128, 4, 128]),
                                                                     op=ALU.mult), r=[e_, msk_], w=[p_])
                                else:
                                    P.act(lambda e: e.activation(out=p_[:, :], in_=sT[:, :], func=AF.Exp), r=[sT], w=[p_])
                                return p_

                            def stage_b(kind, kt, s, h, p_, first, last):
                                Ot = PB[5] if kind == 'sel' else PB[6]
                                vv = vs if kind == 'sel' else vw
                                for g in range(4):
                                    P.pe(lambda e, g=g: e.matmul(Ot[:, g * 65:(g + 1) * 65], lhsT=p_[:, g * 128:(g + 1) * 128], rhs=vv[:, kt, h, :],
                                                                 start=(first and g == 0), stop=last, skip_group_check=True), r=[p_, vv], w=[Ot])
                                if last:
                                    finalize(Ot, h, s, 1 if kind == 'sel' else 2)

                            LOOK = 3
                            rr['An'] = 2
                            rr['A'] = 0
                            pairs = [(s, h) for s in range(4) for h in range(2)]
                            chains = [cmp_chain(s, h) for (s, h) in pairs]
                            for _ in chains[0]:
                                pass
                            for pi, (s, h) in enumerate(pairs):
                                qi = i * 4 + s
                                nxc = chains[pi + 1] if pi + 1 < len(pairs) else None
                                kts = list(range(max(0, qi - 4), qi + 1))
                                items = [('sel', kt, kt == 0, kt == qi) for kt in range(qi + 1)]
                                items += [('win', kt, kt == kts[0], kt == qi) for kt in kts]
                                pend = []
                                for (kind, kt, first, last) in items:
                                    p_ = stage_a(kind, kt, s, h)
                                    pend.append((kind, kt, p_, first, last))
                                    if nxc is not None:
                                        next(nxc, None)
                                    if len(pend) > LOOK:
                                        k_, kt_, pp_, f_, l_ = pend.pop(0)
                                        stage_b(k_, kt_, s, h, pp_, f_, l_)
                                while pend:
                                    k_, kt_, pp_, f_, l_ = pend.pop(0)
                                    stage_b(k_, kt_, s, h, pp_, f_, l_)
                                if nxc is not None:
                                    for _ in nxc:
                                        pass
                                if h == 1:
                                    oa = o_accs[s % 2]
                                    P.act(lambda e, oa=oa: e.activation(out=o_bf[:, :], in_=oa[:, :, :].rearrange("p h d -> p (h d)"), func=AF.Copy),
                                          r=[oa], w=[o_bf])
                                    for c in range(4):
                                        P.pe(lambda e, c=c: e.transpose(out=psT[:, c * 128:(c + 1) * 128], in_=o_bf[:, c * 128:(c + 1) * 128],
                                                                        identity=ident_b[:, :]), r=[o_bf, ident_b], w=[psT])
                                    P.dve(lambda e, s=s: e.tensor_copy(out=oT[:, :, s * 128:(s + 1) * 128],
                                                                       in_=psT[:, 0:512].rearrange("p (c q) -> p c q", c=4)), r=[psT], w=[oT])
                            rr['An'] = 3
                            wto, wvo = wload('nsa_w_o', l, 0, 4, 0, 1024)
                            gv = gate_view(2)
                            for k in range(8):
                                pt = psA()
                                fm_mm(pt, wvo, (k * 128, k * 128 + 128), lambda kk: oT[:, kk, :], 4, rd=[wto, oT])
                                acc_mixed(k, pt[:, :], pt, sc, l, gv, 2)
                        P.barrier()
                    rr['An'] = 7
                    with ExitStack() as sc:
                        mb = P.tile("mixed_bf", [128, 8, TT], BF16, sc)
                        for k in range(8):
                            P.act(lambda e, k=k: e.activation(out=mb[:, k, :], in_=mixed[:, k, :], func=AF.Copy),
                                  r=[mixed], w=[mb])
                        for half in range(2):
                            wt, wv = wload('w_out', l, 0, 8, half * 512, 512)
                            for s in range(4):
                                pt = psA()
                                for k in range(8):
                                    P.pe(lambda e, k=k, s=s, pt=pt, wv=wv: e.matmul(
                                        pt[:, :], lhsT=mb[:, k, s * 128:(s + 1) * 128], rhs=wv[:, k, :],
                                        start=(k == 0), stop=(k == 7)), r=[mb, wt], w=[pt])
                                P.dve(lambda e, s=s, pt=pt, half=half: e.tensor_tensor(
                                    out=xt[:, s, half * 512:(half + 1) * 512], in0=pt[:, :],
                                    in1=xt[:, s, half * 512:(half + 1) * 512], op=ALU.add), r=[pt, xts[s]], w=[xts[s]])
                    P.barrier()
                rr['An'] = 7
                if F_F:
                    with ExitStack() as sc:
                        rmsnorm_T(1, sc)
                        aT = P.tile("aT", [128, 22, TT], BF16, sc)
                        silu_t = [P.tile("silu", [128, TT], F32, sc) for _ in range(2)]
                        for gb in range(6):
                            nt_ = 4 if gb < 5 else 2
                            wtg, wvg = wload('ffn_w_gate_up', l, 0, 8, gb * 512, nt_ * 128)
                            wtu, wvu = wload('ffn_w_gate_up', l, 0, 8, DFF + gb * 512, nt_ * 128)
                            for jj in range(nt_):
                                j = gb * 4 + jj
                                pg = psA()
                                fm_mm(pg, wvg, (jj * 128, jj * 128 + 128), lambda kk: hT[:, kk, :], 8, rd=[wtg, hT])
                                pu = psA()
                                fm_mm(pu, wvu, (jj * 128, jj * 128 + 128), lambda kk: hT[:, kk, :], 8, rd=[wtu, hT])
                                sg = silu_t[j % 2]
                                P.act(lambda e, pg=pg, sg=sg: e.activation(out=sg[:, :], in_=pg[:, :], func=AF.Silu),
                                      r=[pg], w=[sg])
                                P.dve(lambda e, pu=pu, sg=sg, j=j: e.tensor_tensor(out=aT[:, j, :], in0=pu[:, :], in1=sg[:, :],
                                                                                  op=ALU.mult), r=[pu, sg], w=[aT])
                        rr['An'] = 3
                        rr['A'] = 0
                        for half in range(2):
                            accs = [PB[3], PB[4], PB[5], PB[6]]
                            for cg in range(3):
                                nk = 8 if cg < 2 else 6
                                wt, wv = wload('ffn_w_down', l, cg * 1024, nk, half * 512, 512)
                                for s in range(4):
                                    for kk in range(nk):
                                        j = cg * 8 + kk
                                        P.pe(lambda e, s=s, kk=kk, j=j, wv=wv: e.matmul(
                                            accs[s][:, :], lhsT=aT[:, j, s * 128:(s + 1) * 128], rhs=wv[:, kk, :],
                                            start=(j == 0), stop=(j == 21)), r=[aT, wt], w=[accs[s]])
                            for s in range(4):
                                P.dve(lambda e, s=s, half=half: e.tensor_tensor(
                                    out=xt[:, s, half * 512:(half + 1) * 512], in0=accs[s][:, :],
                                    in1=xt[:, s, half * 512:(half + 1) * 512], op=ALU.add), r=[accs[s], xts[s]], w=[xts[s]])
                                if half == 1:
                                    P.dma('act', out_d[t0 + s * 128:t0 + (s + 1) * 128, :], xt[:, s, :], r=[xts[s]], w=[b_xd[i][s]])
                    P.barrier()
                if not F_F:
                    for s_ in range(4):
                        P.dma('act', out_d[t0 + s_ * 128:t0 + (s_ + 1) * 128, :], xt[:, s_, :], r=[xts[s_]], w=[b_xd[i][s_]])
            for i in range(NT):
                step(l, i)
        P.finish('sp')
        nc._reg = P.reg
        print('sbuf remaining', nc.sbuf_bytes_remaining)
        P.emit()
    return nc


def conv_branch(P, sc, l, i, env):
    raise NotImplementedError


_NC_CACHE = {}


def kernel(**inputs):
    x = np.ascontiguousarray(np.asarray(inputs['x'], dtype=np.float32))
    B, S, _ = x.shape
    key = (S, 4)
    if key not in _NC_CACHE:
        _NC_CACHE[key] = build(S, 4, "fcsn")
    nc = _NC_CACHE[key]
    in_maps = []
    for c in range(8):
        m = {'x': x[c % B]}
        for k in PARAM_SHAPES:
            m[k] = np.ascontiguousarray(np.asarray(inputs[k], dtype=np.float32))
        in_maps.append(m)
    res = run_bass_kernel_spmd(nc, in_maps, core_ids=list(range(8)))
    out = np.stack([res.results[b]['out'] for b in range(B)], axis=0)
    return out.astype(np.float32)
```
